# Optimizing a Trainium2 kernel written in Bass

```python
import jax, jax.numpy as jnp
from jax import lax
import numpy as np

D_MODEL = 2048
BATCH = 8
SEQ = 2048
DEPTH = 1

MEM_LEN = 256
NORM_EPS = 1e-6
GMLP_WIDTH = D_MODEL // 2
GMLP_CHUNK = 128
GMLP_GROUPS = 8
GMLP_GROUP_DIM = GMLP_WIDTH // GMLP_GROUPS
MOBA_HEAD_DIM = 128
MOBA_HEADS = (D_MODEL // 2) // MOBA_HEAD_DIM
MOBA_WIDTH = MOBA_HEADS * MOBA_HEAD_DIM
MOBA_BLOCK = 256
MOBA_TOPK = 3
MOBA_Q_CHUNK = 16
ROPE_DIM = MOBA_HEAD_DIM // 4
ROPE_THETA = 500000.0
CA_HEADS = 4
CA_HEAD_DIM = 128
CA_WIDTH = CA_HEADS * CA_HEAD_DIM
FFN_HIDDEN = -(-8 * D_MODEL // (3 * 256)) * 256
IN_COLS = 2 * GMLP_WIDTH + 3 * MOBA_WIDTH + 2 * D_MODEL
IN_SPLITS = [GMLP_WIDTH, 2 * GMLP_WIDTH, 2 * GMLP_WIDTH + MOBA_WIDTH, 2 * GMLP_WIDTH + 2 * MOBA_WIDTH,
             2 * GMLP_WIDTH + 3 * MOBA_WIDTH, 2 * GMLP_WIDTH + 3 * MOBA_WIDTH + D_MODEL]

kernel_name = "hybrid_gmlp_moba_gated_block"


def rms_norm(x, g):
    xf = x.astype(jnp.float32)
    y = xf * lax.rsqrt(jnp.mean(xf * xf, axis=-1, keepdims=True) + NORM_EPS)
    return (y * g.astype(jnp.float32)).astype(x.dtype)


def layer_norm(x, g, b):
    xf = x.astype(jnp.float32)
    mu = jnp.mean(xf, axis=-1, keepdims=True)
    xc = xf - mu
    y = xc * lax.rsqrt(jnp.mean(xc * xc, axis=-1, keepdims=True) + NORM_EPS)
    return (y * g.astype(jnp.float32) + b.astype(jnp.float32)).astype(x.dtype)


def partial_rope(x, positions):
    inv_freq = 1.0 / (ROPE_THETA ** (jnp.arange(0, ROPE_DIM, 2, dtype=jnp.float32) / ROPE_DIM))
    ang = positions.astype(jnp.float32)[..., None] * inv_freq
    cos = jnp.cos(ang)[:, :, None, :]
    sin = jnp.sin(ang)[:, :, None, :]
    xr = x[..., :ROPE_DIM].astype(jnp.float32)
    x1, x2 = xr[..., :ROPE_DIM // 2], xr[..., ROPE_DIM // 2:]
    rot = jnp.concatenate([x1 * cos - x2 * sin, x2 * cos + x1 * sin], axis=-1)
    return jnp.concatenate([rot.astype(x.dtype), x[..., ROPE_DIM:]], axis=-1)


def chunked_spatial_gating(u, v, w_s, b_s):
    B, S, _ = v.shape
    nc = S // GMLP_CHUNK
    causal = jnp.tril(jnp.ones((GMLP_CHUNK, GMLP_CHUNK), dtype=bool))
    w = jnp.where(causal[None], w_s, jnp.zeros((), w_s.dtype))
    vg = v.reshape(B, nc, GMLP_CHUNK, GMLP_GROUPS, GMLP_GROUP_DIM)
    s = jnp.einsum('gts,bcsgd->bctgd', w, vg) + b_s.T[None, None, :, :, None]
    return u * s.reshape(B, S, GMLP_WIDTH)


def moba_attention(q, k, v):
    B, S, H, Dh = q.shape
    nb = -(-S // MOBA_BLOCK)
    pad = nb * MOBA_BLOCK - S
    topk = min(MOBA_TOPK, nb)
    scale = Dh ** -0.5
    q = q.transpose(0, 2, 1, 3)
    k = jnp.pad(k.transpose(0, 2, 1, 3), ((0, 0), (0, 0), (0, pad), (0, 0)))
    v = jnp.pad(v.transpose(0, 2, 1, 3), ((0, 0), (0, 0), (0, pad), (0, 0)))
    k_blocks = k.reshape(B, H, nb, MOBA_BLOCK, Dh)
    v_blocks = v.reshape(B, H, nb, MOBA_BLOCK, Dh)
    k_mean = jnp.mean(k_blocks.astype(jnp.float32), axis=3).astype(k.dtype)
    b_idx = jnp.arange(B)[:, None, None, None]
    h_idx = jnp.arange(H)[None, :, None, None]
    neg = jnp.float32(-jnp.inf)

    def chunk(c):
        start = c * MOBA_Q_CHUNK
        qc = lax.dynamic_slice_in_dim(q, start, MOBA_Q_CHUNK, axis=2)
        q_pos = start + jnp.arange(MOBA_Q_CHUNK)
        own = start // MOBA_BLOCK
        gate = jnp.einsum('bhqd,bhnd->bhqn', qc, k_mean).astype(jnp.float32)
        gate = jnp.where((jnp.arange(nb) < own)[None, None, None, :], gate, neg)
        _, sel = lax.top_k(gate, topk)
        sel_valid = sel < own
        k_sel = k_blocks[b_idx, h_idx, sel]
        v_sel = v_blocks[b_idx, h_idx, sel]
        s_sel = jnp.einsum('bhqd,bhqjkd->bhqjk', qc, k_sel).astype(jnp.float32) * scale
        s_sel = jnp.where(sel_valid[..., None], s_sel, neg).reshape(B, H, MOBA_Q_CHUNK, topk * MOBA_BLOCK)
        k_own = lax.dynamic_slice_in_dim(k, own * MOBA_BLOCK, MOBA_BLOCK, axis=2)
        v_own = lax.dynamic_slice_in_dim(v, own * MOBA_BLOCK, MOBA_BLOCK, axis=2)
        s_own = jnp.einsum('bhqd,bhkd->bhqk', qc, k_own).astype(jnp.float32) * scale
        key_pos = own * MOBA_BLOCK + jnp.arange(MOBA_BLOCK)
        s_own = jnp.where((key_pos[None, :] <= q_pos[:, None])[None, None], s_own, neg)
        p = jax.nn.softmax(jnp.concatenate([s_sel, s_own], axis=-1), axis=-1)
        p_sel = p[..., :topk * MOBA_BLOCK].reshape(B, H, MOBA_Q_CHUNK, topk, MOBA_BLOCK).astype(v.dtype)
        p_own = p[..., topk * MOBA_BLOCK:].astype(v.dtype)
        return (jnp.einsum('bhqjk,bhqjkd->bhqd', p_sel, v_sel)
                + jnp.einsum('bhqk,bhkd->bhqd', p_own, v_own))

    out = lax.map(chunk, jnp.arange(S // MOBA_Q_CHUNK))
    return out.transpose(1, 0, 3, 2, 4).reshape(B, S, H * Dh)


def token_mixer(n, positions, w_in, ln_v_g, ln_v_b, w_spatial, b_spatial, w_branch_a, w_branch_b, w_tm_out):
    B, S, _ = n.shape
    proj = n @ w_in
    u, v, q, k, v_att, g_a, g_b = jnp.split(proj, IN_SPLITS, axis=-1)
    u = jax.nn.gelu(u, approximate=False)
    v = layer_norm(jax.nn.gelu(v, approximate=False), ln_v_g, ln_v_b)
    y_a = chunked_spatial_gating(u, v, w_spatial, b_spatial) @ w_branch_a
    q = partial_rope(q.reshape(B, S, MOBA_HEADS, MOBA_HEAD_DIM), positions)
    k = partial_rope(k.reshape(B, S, MOBA_HEADS, MOBA_HEAD_DIM), positions)
    v_att = v_att.reshape(B, S, MOBA_HEADS, MOBA_HEAD_DIM)
    y_b = moba_attention(q, k, v_att) @ w_branch_b
    merged = jax.nn.sigmoid(g_a) * y_a + jax.nn.sigmoid(g_b) * y_b
    return merged @ w_tm_out


def cross_attention(n, mem_n, w_q, w_kv, w_out):
    B, S, _ = n.shape
    M = mem_n.shape[1]
    q = (n @ w_q).reshape(B, S, CA_HEADS, CA_HEAD_DIM)
    kv = mem_n @ w_kv
    k = kv[..., :CA_WIDTH].reshape(B, M, CA_HEADS, CA_HEAD_DIM)
    v = kv[..., CA_WIDTH:].reshape(B, M, CA_HEADS, CA_HEAD_DIM)
    s = jnp.einsum('bshd,bmhd->bhsm', q, k).astype(jnp.float32) * (CA_HEAD_DIM ** -0.5)
    p = jax.nn.softmax(s, axis=-1).astype(v.dtype)
    o = jnp.einsum('bhsm,bmhd->bshd', p, v).reshape(B, S, CA_WIDTH)
    return o @ w_out


def swiglu(n, w_gate, w_up, w_down):
    return (jax.nn.silu(n @ w_gate) * (n @ w_up)) @ w_down


def setup_inputs(seed: int = 0) -> dict:
    key = jax.random.key(seed)
    ks = jax.random.split(key, 24)
    f32 = jnp.float32

    def dense(k, fan_in, shape):
        return jax.random.normal(k, shape, f32) * fan_in ** -0.5

    def gain(k, dim):
        return 1.0 + 0.02 * jax.random.normal(k, (DEPTH, dim), f32)

    x = jax.random.normal(ks[0], (BATCH, SEQ, D_MODEL), f32)
    mem = jax.random.normal(ks[1], (BATCH, MEM_LEN, D_MODEL), f32)
    positions = (jax.random.randint(ks[2], (BATCH, 1), 0, 1024, dtype=jnp.int32)
                 + jnp.arange(SEQ, dtype=jnp.int32)[None, :])
    return {
        "x": x,
        "mem": mem,
        "positions": positions,
        "tm_pre_g": gain(ks[3], D_MODEL),
        "tm_post_g": gain(ks[4], D_MODEL),
        "w_in": dense(ks[5], D_MODEL, (DEPTH, D_MODEL, IN_COLS)),
        "ln_v_g": gain(ks[6], GMLP_WIDTH),
        "ln_v_b": 0.02 * jax.random.normal(ks[7], (DEPTH, GMLP_WIDTH), f32),
        "w_spatial": dense(ks[8], GMLP_CHUNK, (DEPTH, GMLP_GROUPS, GMLP_CHUNK, GMLP_CHUNK)),
        "b_spatial": 1.0 + 0.02 * jax.random.normal(ks[9], (DEPTH, GMLP_GROUPS, GMLP_CHUNK), f32),
        "w_branch_a": dense(ks[10], GMLP_WIDTH, (DEPTH, GMLP_WIDTH, D_MODEL)),
        "w_branch_b": dense(ks[11], MOBA_WIDTH, (DEPTH, MOBA_WIDTH, D_MODEL)),
        "w_tm_out": dense(ks[12], D_MODEL, (DEPTH, D_MODEL, D_MODEL)),
        "ca_pre_g": gain(ks[13], D_MODEL),
        "ca_post_g": gain(ks[14], D_MODEL),
        "mem_norm_g": gain(ks[15], D_MODEL),
        "w_ca_q": dense(ks[16], D_MODEL, (DEPTH, D_MODEL, CA_WIDTH)),
        "w_ca_kv": dense(ks[17], D_MODEL, (DEPTH, D_MODEL, 2 * CA_WIDTH)),
        "w_ca_out": dense(ks[18], CA_WIDTH, (DEPTH, CA_WIDTH, D_MODEL)),
        "ffn_pre_g": gain(ks[19], D_MODEL),
        "ffn_post_g": gain(ks[20], D_MODEL),
        "w_ffn_gate": dense(ks[21], D_MODEL, (DEPTH, D_MODEL, FFN_HIDDEN)),
        "w_ffn_up": dense(ks[22], D_MODEL, (DEPTH, D_MODEL, FFN_HIDDEN)),
        "w_ffn_down": dense(ks[23], FFN_HIDDEN, (DEPTH, FFN_HIDDEN, D_MODEL)),
    }


def reference(x, mem, positions, tm_pre_g, tm_post_g, w_in, ln_v_g, ln_v_b, w_spatial, b_spatial,
              w_branch_a, w_branch_b, w_tm_out, ca_pre_g, ca_post_g, mem_norm_g, w_ca_q, w_ca_kv, w_ca_out,
              ffn_pre_g, ffn_post_g, w_ffn_gate, w_ffn_up, w_ffn_down):
    h = x
    for l in range(DEPTH):
        n = rms_norm(h, tm_pre_g[l])
        y = token_mixer(n, positions, w_in[l], ln_v_g[l], ln_v_b[l], w_spatial[l], b_spatial[l],
                        w_branch_a[l], w_branch_b[l], w_tm_out[l])
        h = h + rms_norm(y, tm_post_g[l])
        n = rms_norm(h, ca_pre_g[l])
        mem_n = rms_norm(mem, mem_norm_g[l])
        y = cross_attention(n, mem_n, w_ca_q[l], w_ca_kv[l], w_ca_out[l])
        h = h + rms_norm(y, ca_post_g[l])
        n = rms_norm(h, ffn_pre_g[l])
        y = swiglu(n, w_ffn_gate[l], w_ffn_up[l], w_ffn_down[l])
        h = h + rms_norm(y, ffn_post_g[l])
    return h
```

```python
import os
from contextlib import ExitStack

import numpy as np
import ml_dtypes

import concourse.bass as bass
import concourse.mybir as mybir
from concourse.bass_utils import run_bass_kernel_spmd

F32 = mybir.dt.float32
BF16 = mybir.dt.bfloat16
I32 = mybir.dt.int32
ALU = mybir.AluOpType
AF = mybir.ActivationFunctionType
AX = mybir.AxisListType

D = 2048
SEQ = 2048
NT = 16
TG = 512
NTG = 4
KC = 16
MEM = 256
GW = 1024
HID = 5632
HC = 44
EPS = 1e-6
NEG = -30000.0
SEM_LIMIT = 24000
STRICT_SYNC = True


class Buf:
    __slots__ = ("name", "last_w", "rd_eng", "rd_dma", "excl")

    def __init__(self, name, excl=False):
        self.name = name
        self.excl = excl
        self.last_w = None
        self.rd_eng = {}
        self.rd_dma = []


class DSem:
    def __init__(self, sched, name):
        self.s = sched
        self.name = name
        self.sem = None
        self.count = 0
        self.gen = 0
        self.twin = None
        sched.dsems.append(self)

    def bump(self):
        if self.sem is None:
            self.sem = self.s.new_sem(self.name)
        if self.count + 16 > SEM_LIMIT:
            self.s.final_list.append((self.sem, self.count))
            self.gen += 1
            self.sem = self.s.new_sem("%s_g%d" % (self.name, self.gen))
            self.count = 0
        self.count += 16
        return self.sem, self.count


class Op:
    __slots__ = ("eng", "fn", "waits", "dma", "need_inc", "inc", "idx", "dsem", "dval", "dsem_obj")


ENGS = ("pe", "act", "dve", "pool", "sp")


class Sched:
    def __init__(self, nc, stack):
        self.nc = nc
        self.stack = stack
        self.ops = {e: [] for e in ENGS}
        self.nsem = 0
        self.dsems = []
        self.final_list = []

    def new_sem(self, name):
        self.nsem += 1
        return self.stack.enter_context(self.nc.semaphore("s%d_%s" % (self.nsem, name)))

    def _dep(self, op, p, raw):
        if p is None or p is op:
            return
        if p.dma:
            so = p.dsem_obj
            if so.sem is p.dsem:
                op.waits.append(("d", p.dsem, so.count))
            else:
                op.waits.append(("d", p.dsem, p.dval))
            return
        if p.eng == op.eng and not op.dma:
            if p.eng == "pe":
                return
            if not raw and not STRICT_SYNC:
                return
        p.need_inc = True
        op.waits.append(("c", p))

    def add(self, eng, fn, reads=(), writes=(), dma=False, dsem=None):
        op = Op()
        op.eng = eng
        op.fn = fn
        op.waits = []
        op.dma = dma
        op.need_inc = False
        op.inc = None
        op.idx = len(self.ops[eng])
        op.dsem = None
        op.dval = 0
        op.dsem_obj = dsem
        for b in reads:
            self._dep(op, b.last_w, True)
            if b.excl:
                for r in b.rd_eng.values():
                    if r.eng != eng:
                        self._dep(op, r, False)
        for b in writes:
            lw = b.last_w
            if not (dma and lw is not None and lw.dma and lw.dsem_obj is dsem):
                self._dep(op, lw, False)
            for r in b.rd_eng.values():
                self._dep(op, r, False)
            for r in b.rd_dma:
                self._dep(op, r, False)
        for b in writes:
            b.last_w = op
            b.rd_eng = {}
            b.rd_dma = []
        for b in reads:
            if dma:
                b.rd_dma.append(op)
                if len(b.rd_dma) > 8:
                    b.rd_dma = b.rd_dma[-8:]
            else:
                b.rd_eng[eng] = op
        if dma:
            op.dsem, op.dval = dsem.bump()
        self.ops[eng].append(op)
        return op

    def emit(self):
        nc = self.nc
        esem = {}
        ecount = {}
        for e in ENGS:
            esem[e] = self.new_sem("eng_" + e)
            ecount[e] = 0
            for op in self.ops[e]:
                if op.dma or not op.need_inc:
                    continue
                if ecount[e] + 1 > SEM_LIMIT:
                    esem[e] = self.new_sem("eng_" + e)
                    ecount[e] = 0
                ecount[e] += 1
                op.inc = (esem[e], ecount[e])
        fin = list(self.final_list) + [(d.sem, d.count) for d in self.dsems if d.sem is not None]

        def run(e, engine, last=False):
            waited = {}
            for op in self.ops[e]:
                for w in op.waits:
                    if w[0] == "d":
                        sem, val = w[1], w[2]
                    else:
                        sem, val = w[1].inc
                    k = id(sem)
                    if waited.get(k, 0) >= val:
                        continue
                    waited[k] = val
                    engine.wait_ge(sem, val)
                ins = op.fn(engine)
                if op.dma:
                    ins.then_inc(op.dsem, 16)
                elif op.inc is not None:
                    ins.then_inc(op.inc[0], 1)
            if last:
                for (fs, fc) in fin:
                    engine.wait_ge(fs, fc)

        with nc.Block() as block:
            @block.tensor
            def _(eng):
                run("pe", eng)

            @block.scalar
            def _(eng):
                run("act", eng)

            @block.vector
            def _(eng):
                run("dve", eng)

            @block.gpsimd
            def _(eng):
                run("pool", eng)

            @block.sync
            def _(eng):
                run("sp", eng, last=True)


class Builder:
    def __init__(self, nc, stack, dbg=False, stop=None):
        self.nc = nc
        self.st = stack
        self.S = Sched(nc, stack)
        self.dbg = dbg
        self.stop = stop
        self.ps_i = 0

    def sb(self, name, shape, dt):
        return self.st.enter_context(self.nc.sbuf_tensor(name, list(shape), dt))

    def dram_in(self, name, shape, dt):
        return self.nc.dram_tensor(name, list(shape), dt, kind="ExternalInput").ap()

    def dram_out(self, name, shape, dt):
        return self.nc.dram_tensor(name, list(shape), dt, kind="ExternalOutput").ap()

    def scratch(self, name, shape, dt):
        kind = "ExternalOutput" if self.dbg else "Internal"
        return self.nc.dram_tensor(name, list(shape), dt, kind=kind).ap()

    def mm(self, out, lhsT, rhs, start, stop, reads, writes):
        self.S.add("pe", lambda e: e.matmul(out, lhsT, rhs, start=start, stop=stop), reads, writes)

    def tr(self, out, in_, ident, reads, writes):
        self.S.add("pe", lambda e: e.transpose(out, in_, ident), reads, writes)

    def act(self, out, in_, func, reads, writes, scale=None, bias=None, accum=None):
        kw = {}
        if scale is not None:
            kw["scale"] = scale
        if bias is not None:
            kw["bias"] = bias
        if accum is not None:
            kw["accum_out"] = accum
        self.S.add("act", lambda e: e.activation(out, in_, func, **kw), reads, writes)

    def tt(self, out, in0, in1, op, reads, writes, eng="dve"):
        self.S.add(eng, lambda e: e.tensor_tensor(out, in0, in1, op), reads, writes)

    def ts(self, out, in0, s1, s2, op0, op1, reads, writes, eng="dve"):
        if op1 is None:
            self.S.add(eng, lambda e: e.tensor_scalar(out, in0, s1, None, op0), reads, writes)
        else:
            self.S.add(eng, lambda e: e.tensor_scalar(out, in0, s1, s2, op0, op1), reads, writes)

    def stt(self, out, in0, scalar, in1, op0, op1, reads, writes, accum=None):
        if accum is None:
            self.S.add("dve", lambda e: e.scalar_tensor_tensor(out, in0, scalar, in1, op0, op1), reads, writes)
        else:
            self.S.add("dve", lambda e: e.scalar_tensor_tensor(out, in0, scalar, in1, op0, op1, accum_out=accum),
                       reads, writes)

    def cp(self, out, in_, reads, writes, eng="dve"):
        if eng == "act":
            self.S.add("act", lambda e: e.copy(out, in_), reads, writes)
        else:
            self.S.add(eng, lambda e: e.tensor_copy(out, in_), reads, writes)

    def dma(self, q, out, in_, dsem, reads, writes):
        if q != "sp":
            if dsem.twin is None:
                dsem.twin = DSem(self.S, dsem.name + "_sw")
            dsem = dsem.twin
        return self.S.add(q, lambda e: e.dma_start(out=out, in_=in_), reads, writes, dma=True, dsem=dsem)

    def dsem(self, name):
        return DSem(self.S, name)


def build_program(dbg=False, stop=None):
    nc = bass.Bass("TRN2", target_bir_lowering=False)
    with ExitStack() as st:
        B = Builder(nc, st, dbg, stop)
        _program(B)
        B.S.emit()
    return nc


PHASES = ("p0", "p1a", "p1b", "p1c", "p1d", "p1e", "p2a", "p2b", "p3a", "p3b", "p3c")


def _program(B):
    nc = B.nc
    S = B.S
    stop = B.stop
    PI = float(np.pi)

    x = B.dram_in("x", [SEQ, D], F32)
    mem = B.dram_in("mem", [MEM, D], F32)
    pos = B.dram_in("pos", [1, SEQ], I32)
    w_in = B.dram_in("w_in", [D, 9216], F32)
    w_sp = B.dram_in("w_sp", [8, 128, 128], F32)
    b_sp = B.dram_in("b_sp", [1, 1024], F32)
    w_a = B.dram_in("w_a", [GW, D], F32)
    w_b = B.dram_in("w_b", [GW, D], F32)
    w_tm = B.dram_in("w_tm", [D, D], F32)
    w_cq = B.dram_in("w_cq", [D, 512], F32)
    w_ckv = B.dram_in("w_ckv", [D, 1024], F32)
    w_co = B.dram_in("w_co", [512, D], F32)
    w_fg = B.dram_in("w_fg", [D, HID], F32)
    w_fu = B.dram_in("w_fu", [D, HID], F32)
    w_fd = B.dram_in("w_fd", [HID, D], F32)
    gains = B.dram_in("gains", [3, D], F32)
    gcols_d = B.dram_in("gcols", [128, 64], F32)
    lnv = B.dram_in("lnv", [1, 2 * GW], F32)
    c_identb = B.dram_in("c_identb", [128, 128], BF16)
    c_identf = B.dram_in("c_identf", [128, 128], F32)
    c_onesb = B.dram_in("c_onesb", [128, 128], BF16)
    c_tri = B.dram_in("c_tri", [128, 128], F32)
    c_cm = B.dram_in("c_cm", [128, 4 * 512], BF16)
    c_eall = B.dram_in("c_eall", [8, 8 * 128], BF16)
    c_ib = B.dram_in("c_ib", [128, 128], F32)
    c_nv = B.dram_in("c_nv", [128, 128], F32)
    c_col = B.dram_in("c_col", [128, 4], F32)
    c_perm = B.dram_in("c_perm", [128, 32], F32)
    out = B.dram_out("out", [SEQ, D], F32)

    CS = B.scratch("CS", [2, 128, SEQ], F32)
    MA = B.scratch("MA", [KC, 128, SEQ], F32)
    MG = B.scratch("MG", [NT, 128, KC * 128], BF16)
    H1 = B.scratch("H1", [SEQ, D], F32)
    H2 = B.scratch("H2", [SEQ, D], F32)
    HIDT = B.scratch("HIDT", [NT, 128, HC * 128], BF16)
    Y3 = B.scratch("Y3", [SEQ, 1024], F32)
    DBG = B.dram_out("DBG", [128, 32768], F32) if B.dbg else None

    bCS = Buf("CS")
    bMA = [[Buf("MA") for _ in range(NTG)] for _ in range(KC)]
    bMG = [[Buf("MG") for _ in range(NTG)] for _ in range(KC)]
    bH1 = [Buf("H1") for _ in range(NT)]
    bH2 = [Buf("H2") for _ in range(NT)]
    bHID = [[Buf("HID") for _ in range(NTG)] for _ in range(HC)]
    bY3 = [Buf("Y3") for _ in range(NT)]

    RA = B.sb("RA", [128, 32768], BF16)
    RB = B.sb("RB", [128, 32768], BF16)
    gRA = [[Buf("RA") for _ in range(16)] for _ in range(16)]
    gRB = [[Buf("RB") for _ in range(16)] for _ in range(16)]

    def fb(grid, start, length):
        res = []
        s0 = start // 128
        s1 = (start + length - 1) // 128
        for s in range(s0, s1 + 1):
            res.append(grid[s // 16][s % 16])
        return res

    def fb2(grid, start, nrows, rstride, length):
        res = []
        for r in range(nrows):
            res.extend(fb(grid, start + r * rstride, length))
        return res

    NSLOT = 2
    SLOTN = 4096
    slots = [B.sb("wslot%d" % i, [128, SLOTN], BF16) for i in range(NSLOT)]
    bslot = [Buf("wslot%d" % i) for i in range(NSLOT)]
    dslot = [B.dsem("wslot%d" % i) for i in range(NSLOT)]
    slot_i = [0]

    identb = B.sb("identb", [128, 128], BF16)
    identf = B.sb("identf", [128, 128], F32)
    onesb = B.sb("onesb", [128, 128], BF16)
    tri = B.sb("tri", [128, 128], F32)
    cm = B.sb("cm", [128, 4, 512], BF16)
    eall = B.sb("eall", [8, 8, 128], BF16)
    ibt = B.sb("ibt", [128, 16, 8], F32)
    nvt = B.sb("nvt", [128, 16, 8], F32)
    ccol = B.sb("ccol", [128, 4], F32)
    permf = B.sb("permf", [128, 32], F32)
    mhalf = B.sb("mhalf", [128, 1], F32)
    itile = B.sb("itile", [128, 256], I32)
    bit = Buf("itile")
    dit = B.dsem("itile")
    bmh = Buf("mhalf")
    B.S.add("pool", lambda e: e.memset(mhalf[:], -0.5), [], [bmh])
    gcols = B.sb("gcols_sb", [128, 4, 16], F32)
    gtab = B.sb("gtab", [128, D], F32)
    bgt = Buf("gtab")
    bconst = Buf("const")
    dconst = B.dsem("const")
    for (dst, src) in ((identb[:], c_identb), (identf[:], c_identf), (onesb[:], c_onesb), (tri[:], c_tri),
                       (cm[:].rearrange("p a b -> p (a b)"), c_cm),
                       (eall[:].rearrange("p a b -> p (a b)"), c_eall),
                       (ibt[:].rearrange("p a b -> p (a b)"), c_ib),
                       (nvt[:].rearrange("p a b -> p (a b)"), c_nv), (ccol[:], c_col), (permf[:], c_perm),
                       (gcols[:].rearrange("p a b -> p (a b)"), gcols_d)):
        B.dma("sp", dst, src, dconst, [], [bconst])

    def load_gtab(src_ap):
        B.dma("sp", gtab[:], src_ap.partition_broadcast(128), dconst, [], [bgt])

    psum = [B.st.enter_context(nc.psum_tensor("ps%d" % i, [128, 512], F32)) for i in range(8)]
    bps = [Buf("ps%d" % i, excl=True) for i in range(8)]

    def ps_next():
        i = B.ps_i % 8
        B.ps_i += 1
        return psum[i], bps[i]

    class Rot:
        def __init__(self, name, shape, dt, n):
            self.t = [B.sb("%s%d" % (name, i), shape, dt) for i in range(n)]
            self.b = [Buf("%s%d" % (name, i)) for i in range(n)]
            self.d = [B.dsem("%s%d" % (name, i)) for i in range(n)]
            self.i = 0

        def next(self):
            i = self.i % len(self.t)
            self.i += 1
            return self.t[i], self.b[i], self.d[i]

    xt = Rot("xt", [128, D], F32, 2)
    yt = Rot("yt", [128, D], F32, 1)
    nb = Rot("nb", [128, D], BF16, 2)
    st32 = Rot("st32", [128, TG], F32, 3)
    stb = Rot("stb", [128, TG], BF16, 3)
    sm = Rot("sm", [128, 16], F32, 8)
    mb = Rot("mb", [128, 128], BF16, 2)
    gmt = Rot("gmt", [128, 128], F32, 2)
    cmpt = yt.t[0][:, 0:1024]
    bcmp = yt.b[0]
    mbt = nb.t[0]
    bmbt = [nb.b[0] for _ in range(NTG)]
    kmt = B.sb("kmt", [128, 16], BF16)
    bkmt = Buf("kmt")
    wspT = B.sb("wspT", [128, 8, 128], BF16)
    bwsp = Buf("wspT")
    bsr = nb.t[1][0:1, :].rearrange("p (a b) -> p a b", a=2)
    bbsr = nb.b[1]

    def wload_into(i, off, W, r0, nrows, c0, ncols):
        kc = nrows // 128
        v = slots[i][:, off:off + kc * ncols].rearrange("p (k n) -> p k n", k=kc)
        src = W[r0:r0 + nrows, c0:c0 + ncols].rearrange("(k p) n -> p k n", p=128)
        B.dma("pool", v, src, dslot[i], [], [bslot[i]])
        return v

    def wload(specs):
        i = slot_i[0] % NSLOT
        slot_i[0] += 1
        off = 0
        views = []
        for (W, r0, nrows, c0, ncols) in specs:
            views.append(wload_into(i, off, W, r0, nrows, c0, ncols))
            off += (nrows // 128) * ncols
        assert off <= SLOTN
        return views, bslot[i]

    def cast_load(dst_ap, W, r0, nrows, c0, ncols, dsem, wbufs):
        src = W[r0:r0 + nrows, c0:c0 + ncols].rearrange("(k p) n -> p k n", p=128)
        B.dma("pool", dst_ap, src, dsem, [], wbufs)

    def dbg_dump(region, grid, nel):
        for c in range(nel // TG):
            s_, sb_, sd = st32.next()
            B.cp(s_[:], region[:, c * TG:(c + 1) * TG], fb(grid, c * TG, TG), [sb_])
            B.dma("sp", DBG[:, c * TG:(c + 1) * TG], s_[:], sd, [sb_], [Buf("d")])

    def rstd_of(ss_ap, ss_buf, ncols, dim):
        t, b, _ = sm.next()
        if ncols > 1:
            B.S.add("dve", lambda e: e.tensor_reduce(t[:, 0:1], ss_ap, AX.X, ALU.add), [ss_buf], [b])
            B.ts(t[:, 1:2], t[:, 0:1], 1.0 / dim, EPS, ALU.mult, ALU.add, [b], [b])
        else:
            B.ts(t[:, 1:2], ss_ap, 1.0 / dim, EPS, ALU.mult, ALU.add, [ss_buf], [b])
        B.tt(t[:, 2:3], t[:, 1:2], mhalf[:, 0:1], ALU.pow, [b, bmh], [b], eng="pool")
        return t[:, 2:3], b

    def prenorm_stats(h_ap, h_buf, n, nbuf, dve_sq=False):
        s, sbf, _ = sm.next()
        if dve_sq:
            B.stt(n[:], h_ap, 1.0, h_ap, ALU.mult, ALU.mult, [h_buf], [nbuf, sbf], accum=s[:, 0:1])
        else:
            B.act(n[:], h_ap, AF.Square, [h_buf], [nbuf, sbf], accum=s[:, 0:1])
        r, rb = rstd_of(s[:, 0:1], sbf, 1, D)
        B.act(n[:], h_ap, AF.Identity, [h_buf, rb], [nbuf], scale=r)

    def transpose_out(n, nbuf, gi, dst, dgrid, dst_off, dst_rstride, tok0, banks=None):
        for half in range(2):
            p, pb = ps_next() if banks is None else banks[half]
            pv = p[:].bitcast(BF16)
            for j in range(8):
                kc = half * 8 + j
                B.tr(pv[:, j * 128:(j + 1) * 128], n[:, kc * 128:(kc + 1) * 128], identb[:], [nbuf, bconst], [pb])
            o = dst[:, dst_off + half * 8 * dst_rstride: dst_off + (half + 1) * 8 * dst_rstride]
            o = o.rearrange("p (k t) -> p k t", k=8)[:, :, tok0:tok0 + 128]
            wb = fb2(dgrid, dst_off + half * 8 * dst_rstride + tok0, 8, dst_rstride, 128)
            g = gcols[:, gi, half * 8:half * 8 + 8].unsqueeze(2).to_broadcast([128, 8, 128])
            B.tt(o, pv.rearrange("p (k t) -> p k t", k=8), g, ALU.mult, [pb, bconst], wb)

    def prenorm_T(h_ap, h_buf, gi, dst, dgrid, dst_off, dst_rstride, tok0):
        n, nbuf, _ = nb.next()
        prenorm_stats(h_ap, h_buf, n, nbuf)
        transpose_out(n, nbuf, gi, dst, dgrid, dst_off, dst_rstride, tok0)

    def post_evac(pss, y, yb, junk, junkb, col0=0):
        s, sbf, _ = sm.next()
        for c, (p, pb) in enumerate(pss):
            B.act(junk[:, c * 512:(c + 1) * 512], p[:], AF.Square, [pb], [junkb, sbf], accum=s[:, c:c + 1])
            B.cp(y[:, col0 + c * 512: col0 + (c + 1) * 512], p[:], [pb], [yb])
        return s, sbf

    def post_apply(s, sbf, ncols, y, yb, res_ap, res_buf):
        r, rb = rstd_of(s[:, 0:ncols], sbf, ncols, D)
        B.stt(y, y, r, gtab[:], ALU.mult, ALU.mult, [yb, rb, bgt], [yb])
        B.tt(res_ap, y, res_ap, ALU.add, [yb, res_buf], [res_buf])

    def rope_tables():
        a, ab, ad = xt.next()
        c_, cb_, cd = xt.next()
        w_, wb_, _ = yt.next()
        for ch in range(8):
            sl = slice(ch * 256, (ch + 1) * 256)
            B.dma("sp", itile[:], pos[:, sl].partition_broadcast(128), dit, [], [bit])
            B.cp(c_[:, sl], itile[:], [bit], [cb_])
        B.ts(w_[:], c_[:], ccol[:, 2:3], None, ALU.mult, None, [cb_, bconst], [wb_])
        B.stt(c_[:], c_[:], ccol[:, 0:1], w_[:], ALU.mult, ALU.add, [cb_, bconst, wb_], [cb_])

        def sin_of(shift, sign_col, dst_row, dsem_):
            B.ts(w_[:], c_[:], shift, 1.0 / (2 * PI), ALU.add, ALU.mult, [cb_], [wb_])
            for ch in range(8):
                sl = slice(ch * 256, (ch + 1) * 256)
                B.cp(itile[:], w_[:, sl], [wb_], [bit])
                B.cp(w_[:, sl], itile[:], [bit], [wb_])
            B.ts(a[:], c_[:], shift, None, ALU.add, None, [cb_], [ab])
            B.stt(a[:], w_[:], -2 * PI, a[:], ALU.mult, ALU.add, [wb_, ab], [ab])
            B.ts(w_[:], a[:], PI, -2 * PI, ALU.is_gt, ALU.mult, [ab], [wb_])
            B.tt(a[:], a[:], w_[:], ALU.add, [ab, wb_], [ab])
            B.ts(a[:], a[:], -PI, PI, ALU.max, ALU.min, [ab], [ab])
            B.act(a[:], a[:], AF.Sin, [ab], [ab])
            if sign_col is not None:
                B.ts(a[:], a[:], ccol[:, sign_col:sign_col + 1], None, ALU.mult, None, [ab, bconst], [ab])
            B.dma("sp", CS[dst_row], a[:], dsem_, [ab], [bCS])

        sin_of(0.0, None, 1, ad)
        sin_of(0.5 * PI, None, 0, ad)


    def load_tile(src_ap, rbufs):
        t, tb, td = xt.next()
        B.dma("sp", t[:], src_ap, td, rbufs, [tb])
        return t, tb, td

    load_gtab(lnv)
    WV0 = 8 * 2048
    WV3 = RB[:, WV0:WV0 + 16384].rearrange("p (k n) -> p k n", k=KC)
    for hf in range(2):
        for kh in range(2):
            cast_load(WV3[:, kh * 8:(kh + 1) * 8, hf * 512:(hf + 1) * 512], w_in, kh * 1024, 1024, 1024 + hf * 512, 512,
                      B.dsem("wv%d%d" % (hf, kh)),
                      [b_ for kc in range(kh * 8, kh * 8 + 8) for b_ in fb(gRB, WV0 + kc * 1024 + hf * 512, 512)])
    wl, wlb, wld = xt.next()
    wv3 = wl[:, 0:1024].rearrange("p (g s) -> p g s", g=8)
    B.dma("sp", wv3, w_sp.rearrange("g t s -> t g s"), wld, [], [wlb])
    for half in range(2):
        p, pb = ps_next()
        for j in range(4):
            g = half * 4 + j
            B.tr(p[:, j * 128:(j + 1) * 128], wv3[:, g, :], identf[:], [wlb, bconst], [pb])
        B.tt(wspT[:, half * 4:half * 4 + 4, :], p[:].rearrange("p (g t) -> p g t", g=4),
             tri[:].unsqueeze(1).to_broadcast([128, 4, 128]), ALU.mult, [pb, bconst], [bwsp])
    ybufs0 = [(yt.t[0], yt.b[0]), (slots[0][:].bitcast(F32), bslot[0])]

    p1a_ps = {}

    def p1a_pe(tt):
            pss = [ps_next(), ps_next()]
            p1a_ps[tt] = pss
            for hf in range(2):
                p, pb = pss[hf]
                for kc in range(KC):
                    B.mm(p[:], RA[:, kc * 2048 + tt * 128: kc * 2048 + tt * 128 + 128],
                         RB[:, WV0 + kc * 1024 + hf * 512: WV0 + kc * 1024 + hf * 512 + 512],
                         kc == 0, kc == KC - 1,
                         [gRA[kc][tt]] + fb(gRB, WV0 + kc * 1024 + hf * 512, 512), [pb])

    def p1a_epi(tt):
            pss = p1a_ps.pop(tt)
            y, yb = ybufs0[tt % 2]
            s, sbf, _ = sm.next()
            for hf in range(2):
                p, pb = pss[hf]
                B.act(y[:, hf * 512:(hf + 1) * 512], p[:], AF.Gelu, [pb], [yb, sbf], accum=s[:, hf:hf + 1])
            B.act(y[:, 1024:2048], y[:, 0:1024], AF.Square, [yb], [yb, sbf], accum=s[:, 2:3])
            B.tt(s[:, 3:4], s[:, 0:1], s[:, 1:2], ALU.add, [sbf], [sbf])
            B.ts(s[:, 4:5], s[:, 3:4], 1.0 / GW, None, ALU.mult, None, [sbf], [sbf])
            B.tt(s[:, 5:6], s[:, 4:5], s[:, 4:5], ALU.mult, [sbf], [sbf])
            B.stt(s[:, 6:7], s[:, 2:3], 1.0 / GW, s[:, 5:6], ALU.mult, ALU.subtract, [sbf], [sbf])
            B.ts(s[:, 9:10], s[:, 6:7], EPS, None, ALU.add, None, [sbf], [sbf])
            B.tt(s[:, 7:8], s[:, 9:10], mhalf[:, 0:1], ALU.pow, [sbf, bmh], [sbf], eng="pool")
            B.stt(s[:, 8:9], s[:, 4:5], -1.0, s[:, 7:8], ALU.mult, ALU.mult, [sbf], [sbf])
            p1a_ps[("s", tt)] = (y, yb, s, sbf)

    def p1a_apply(tt):
            y, yb, s, sbf = p1a_ps.pop(("s", tt))
            B.act(y[:, 1024:2048], y[:, 0:1024], AF.Identity, [yb, sbf], [yb], scale=s[:, 7:8], bias=s[:, 8:9])
            B.tt(y[:, 1024:2048], y[:, 1024:2048], gtab[:, 0:1024], ALU.mult, [yb, bgt], [yb])
            B.tt(RB[:, tt * 1024:(tt + 1) * 1024], y[:, 1024:2048], gtab[:, 1024:2048], ALU.add, [yb, bgt],
                 fb(gRB, tt * 1024, 1024))

    xs = {}

    def p0_load(tt):
        i = tt % 2
        B.dma("sp", xt.t[i][:], x[tt * 128:(tt + 1) * 128, :], xt.d[i], [], [xt.b[i]])

    def p0_sq(tt):
        i = tt % 2
        n, nbuf, _ = nb.next()
        s, sbf, _ = sm.next()
        B.act(n[:], xt.t[i][:], AF.Square, [xt.b[i]], [nbuf, sbf], accum=s[:, 0:1])
        xs[tt] = (n, nbuf, rstd_of(s[:, 0:1], sbf, 1, D))

    def p0_id(tt):
        i = tt % 2
        n, nbuf, (r, rb) = xs[tt]
        B.act(n[:], xt.t[i][:], AF.Identity, [xt.b[i], rb], [nbuf], scale=r)
        if tt + 2 < NT:
            p0_load(tt + 2)

    def p0_tr(tt):
        n, nbuf, _ = xs.pop(tt)
        transpose_out(n, nbuf, 0, RA, gRA, 0, 2048, tt * 128)

    p0_load(0)
    p0_load(1)
    for t0_ in range(2):
        p0_sq(t0_)
        p0_id(t0_)
        p0_tr(t0_)
    p0_sq(2)
    for tt in range(NT):
        p1a_pe(tt)
        if tt + 2 < NT:
            p0_id(tt + 2)
            p0_tr(tt + 2)
        p1a_epi(tt)
        if tt + 3 < NT:
            p0_sq(tt + 3)
        p1a_apply(tt)
    bl, blb, bld = st32.next()
    bl2, bl2b, _ = st32.next()
    B.dma("sp", bl[0:1, 0:512], b_sp[:, 0:512], bld, [], [blb])
    B.dma("sp", bl2[0:1, 0:512], b_sp[:, 512:1024], bld, [], [bl2b])
    for hh, (t_, tb_) in enumerate(((bl, blb), (bl2, bl2b))):
        B.cp(bsr[0:1, 0, hh * 512:(hh + 1) * 512], t_[0:1, 0:512], [tb_], [bbsr])
        B.tt(t_[0:1, 0:512], t_[0:1, 0:512], bsr[0:1, 0, hh * 512:(hh + 1) * 512], ALU.subtract, [tb_, bbsr], [tb_])
        B.cp(bsr[0:1, 1, hh * 512:(hh + 1) * 512], t_[0:1, 0:512], [tb_], [bbsr])

    if stop == "p1a":
        dbg_dump(RB, gRB, 16384)
        return

    AT0 = 8 * 2048

    def wu_load(blk):
        return wload([(w_in, 0, D, blk * 256, 256)])
    nw = wu_load(0)
    for blk in range(4):
        (wv_,), wb_ = nw
        if blk + 1 < 4:
            nw = wu_load(blk + 1)
        for tg in range(NTG):
            for jj in range(2):
                g = blk * 2 + jj
                pu, pub = ps_next()
                for kc in range(KC):
                    B.mm(pu[:], wv_[:, kc, jj * 128:(jj + 1) * 128], RA[:, kc * 2048 + tg * TG: kc * 2048 + (tg + 1) * TG],
                         kc == 0, kc == KC - 1, [wb_] + gRA[kc][tg * 4:tg * 4 + 4], [pub])
                psx, psb = ps_next()
                for hl in range(2):
                    B.mm(psx[:], onesb[0:1, :],
                         bsr[0:1, hl, g * 128:(g + 1) * 128].unsqueeze(1).to_broadcast([1, 4, 128]),
                         hl == 0, False, [bconst, bbsr], [psb])
                for c in range(4):
                    tt = tg * 4 + c
                    B.mm(psx[:, c * 128:(c + 1) * 128], RB[:, tt * 1024 + g * 128: tt * 1024 + (g + 1) * 128],
                         wspT[:, g, :], False, c == 3, fb(gRB, tt * 1024 + g * 128, 128) + [bwsp], [psb])
                ug, ugb, _ = st32.next()
                B.act(ug[:], pu[:], AF.Gelu, [pub], [ugb])
                B.tt(RB[:, AT0 + g * 2048 + tg * TG: AT0 + g * 2048 + (tg + 1) * TG], ug[:], psx[:], ALU.mult,
                     [ugb, psb], fb(gRB, AT0 + g * 2048 + tg * TG, TG))
    if stop == "p1b":
        dbg_dump(RB, gRB, 32768)
        return

    hw = {}

    def load_head_qk(h):
        (wq_,), wqkb = wload([(w_in, 0, D, 2048 + h * 128, 128)])
        i_qk = (slot_i[0] - 1) % NSLOT
        wk_ = wload_into(i_qk, 2048, w_in, 0, D, 3072 + h * 128, 128)
        hw[h] = (wq_, wk_, wqkb)

    def load_head_v(h):
        (wvv,), wvb = wload([(w_in, 0, D, 4096 + h * 128, 128)])
        hw[h] = hw[h] + (wvv, wvb)

    def load_head_w(h):
        load_head_qk(h)
        load_head_v(h)

    def branch_ld(Wbr, gate_c0, j):
        return wload([(w_in, 0, D, gate_c0 + j * 128, 128), (Wbr, 0, GW, j * 128, 128)])

    def branch(Wbr, gate_c0, act_off, second, preloaded=None, tail_hook=None):
        def ld(j):
            return branch_ld(Wbr, gate_c0, j)
        nw = preloaded if preloaded is not None else ld(0)
        for j in range(KC):
            (wg_, wy_), wb_ = nw
            if j + 1 < KC:
                nw = ld(j + 1)
            elif tail_hook is not None:
                tail_hook()
            for tg in range(NTG):
                if second:
                    mt_, mtb, mtd = st32.next()
                    B.dma("sp", mt_[:], MA[j, :, tg * TG:(tg + 1) * TG], mtd, [bMA[j][tg]], [mtb])
                py, pyb = ps_next()
                for kc in range(8):
                    B.mm(py[:], wy_[:, kc, :], RB[:, act_off + kc * 2048 + tg * TG: act_off + kc * 2048 + (tg + 1) * TG],
                         kc == 0, kc == 7, [wb_] + fb(gRB, act_off + kc * 2048 + tg * TG, TG), [pyb])
                pg, pgb = ps_next()
                for kc in range(KC):
                    B.mm(pg[:], wg_[:, kc, :], RA[:, kc * 2048 + tg * TG: kc * 2048 + (tg + 1) * TG],
                         kc == 0, kc == KC - 1, [wb_] + gRA[kc][tg * 4:tg * 4 + 4], [pgb])
                sg, sgb, sgd = st32.next()
                B.act(sg[:], pg[:], AF.Sigmoid, [pgb], [sgb])
                if not second:
                    B.tt(sg[:], sg[:], py[:], ALU.mult, [sgb, pyb], [sgb])
                    B.dma("pool", MA[j, :, tg * TG:(tg + 1) * TG], sg[:], sgd, [sgb], [bMA[j][tg]])
                else:
                    B.tt(sg[:], sg[:], py[:], ALU.mult, [sgb, pyb], [sgb])
                    ob, obb, obd = stb.next()
                    B.tt(ob[:], sg[:], mt_[:], ALU.add, [sgb, mtb], [obb])
                    B.dma("pool", MG[tg * 4:(tg + 1) * 4, :, j * 128:(j + 1) * 128].rearrange("c p t -> p c t"),
                          ob[:].rearrange("p (c t) -> p c t", c=4), obd, [obb], [bMG[j][tg]])

    rope_tables()
    branch(w_a, 5120, AT0, False, tail_hook=lambda: load_head_qk(0))
    if stop == "p1c":
        return

    cosT, bcos = xt.t[0], xt.b[0]
    sinT, bsin = xt.t[1], xt.b[1]
    B.dma("sp", cosT[:], CS[0], xt.d[0], [bCS], [bcos])
    B.dma("sp", sinT[:], CS[1], xt.d[1], [bCS], [bsin])
    SCALE = float(128 ** -0.5)
    mbts = [(nb.t[0], nb.b[0]), (nb.t[1], nb.b[1])]
    kmts = [(kmt[:, 0:8], bkmt), (kmt[:, 8:16], Buf("kmt1"))]
    pj_i = [0]

    def pj_bank():
        i = 5 + pj_i[0] % 3
        pj_i[0] += 1
        return psum[i], bps[i]

    def offs(h):
        s = h % 2
        return (8 + 3 * s) * 2048, (9 + 3 * s) * 2048, (10 + 3 * s) * 2048

    def proj(h, tg):
        wq_, wk_, wqkb, wvv, wvb = hw[h]
        QO, KO, VO = offs(h)
        for (wmat, dst0) in ((wq_, QO), (wk_, KO)):
            pq, pqb = pj_bank()
            for kc in range(KC):
                B.mm(pq[:], wmat[:, kc, :], RA[:, kc * 2048 + tg * TG: kc * 2048 + (tg + 1) * TG],
                     kc == 0, kc == KC - 1, [wqkb] + gRA[kc][tg * 4:tg * 4 + 4], [pqb])
            dst = RB[:, dst0 + tg * TG: dst0 + (tg + 1) * TG]
            dstb = fb(gRB, dst0 + tg * TG, TG)
            qf, qfb, _ = st32.next()
            B.cp(qf[0:32, :], pq[0:32, :], [pqb], [qfb], eng="act")
            B.cp(dst, pq[:], [pqb], dstb, eng="act")
            pr, prb = pj_bank()
            B.mm(pr[0:32, :], permf[0:32, :], qf[0:32, :], True, True, [bconst, qfb], [prb])
            t1, t1b, _ = st32.next()
            B.tt(t1[0:32, :], pq[0:32, :], cosT[0:32, tg * TG:(tg + 1) * TG], ALU.mult, [pqb, bcos], [t1b])
            B.tt(qf[0:32, :], pr[0:32, :], sinT[0:32, tg * TG:(tg + 1) * TG], ALU.mult, [prb, bsin, qfb], [qfb])
            B.tt(dst[0:32, :], t1[0:32, :], qf[0:32, :], ALU.add, [t1b, qfb], dstb)
        pv_, pvb = pj_bank()
        for c in range(4):
            tt = tg * 4 + c
            for kc in range(KC):
                B.mm(pv_[:, c * 128:(c + 1) * 128], RA[:, kc * 2048 + tt * 128: kc * 2048 + tt * 128 + 128],
                     wvv[:, kc, :], kc == 0, kc == KC - 1, [gRA[kc][tt], wvb], [pvb])
        B.cp(RB[:, VO + tg * TG: VO + (tg + 1) * TG], pv_[:], [pvb], fb(gRB, VO + tg * TG, TG), eng="act")

    gstate = {}

    def gate_a(h):
        QO, KO, VO = offs(h)
        kmt_, kmtb = kmts[h % 2]
        kms, kmsb, _ = sm.next()
        B.S.add("dve", lambda e, kms=kms, KO=KO: e.tensor_reduce(
            kms[:, 0:8], RB[:, KO:KO + 2048].rearrange("p (n k) -> p n k", n=8), AX.X, ALU.add), fb(gRB, KO, 2048), [kmsb])
        B.ts(kmt_, kms[:, 0:8], 1.0 / 256.0, None, ALU.mult, None, [kmsb], [kmtb])
        pgt, pgtb = pj_bank()
        for qt in range(NT):
            B.mm(pgt[:, qt * 8:(qt + 1) * 8], RB[:, QO + qt * 128: QO + (qt + 1) * 128], kmt_, True, True,
                 fb(gRB, QO + qt * 128, 128) + [kmtb], [pgtb])
        gm, gmb, _ = gmt.next()
        B.tt(gm[:], pgt[:, 0:128], ibt[:].rearrange("p a b -> p (a b)"), ALU.add, [pgtb, bconst], [gmb])
        gm3 = gm[:].rearrange("p (a b) -> p a b", a=16)
        B.tt(cmpt.rearrange("p (a n m) -> p a n m", a=16, n=8),
             gm3.unsqueeze(2).to_broadcast([128, 16, 8, 8]), gm3.unsqueeze(3).to_broadcast([128, 16, 8, 8]),
             ALU.is_gt, [gmb], [bcmp])
        cnt, cntb, _ = gmt.next()
        B.S.add("dve", lambda e, cnt=cnt: e.tensor_reduce(cnt[:].rearrange("p (a n) -> p a n", a=16),
                                                          cmpt.rearrange("p (a n m) -> p a n m", a=16, n=8),
                                                          AX.X, ALU.add), [bcmp], [cntb])
        mbb, mbbb, _ = mb.next()
        B.stt(mbb[:], cnt[:], 3.0, nvt[:].rearrange("p a b -> p (a b)"), ALU.is_ge, ALU.mult, [cntb, bconst], [mbbb])
        gstate[h] = (mbb, mbbb)

    def gate_b(h):
        mbb, mbbb = gstate[h]
        mbt_, mbtb = mbts[h % 2]
        for tg in range(NTG):
            pt_, ptb = pj_bank()
            ptv = pt_[:].bitcast(BF16)
            for c in range(4):
                qt = tg * 4 + c
                B.tr(ptv[0:8, c * 128:(c + 1) * 128], mbb[:, qt * 8:(qt + 1) * 8], identb[:], [mbbb, bconst], [ptb])
            B.cp(mbt_[0:8, tg * TG:(tg + 1) * TG], ptv[0:8, 0:512], [ptb], [mbtb])

    sc_i = [0]

    def att(h, tg):
        QO, KO, VO = offs(h)
        mbt_, mbtb = mbts[h % 2]
        po, pob = psum[0], bps[0]
        pd, pdb = psum[1], bps[1]
        nkt = 4 * tg + 4

        def score(kt):
            i = 2 + sc_i[0] % 3
            sc_i[0] += 1
            ps_, psb_ = psum[i], bps[i]
            diag = kt >= 4 * tg
            sel = tg >= 2
            B.mm(ps_[:], RB[:, KO + kt * 128: KO + (kt + 1) * 128], RB[:, QO + tg * TG: QO + (tg + 1) * TG], True,
                 not (sel or diag), fb(gRB, KO + kt * 128, 128) + fb(gRB, QO + tg * TG, TG), [psb_])
            if sel:
                B.mm(ps_[:], eall[0:8, kt // 2, :], mbt_[0:8, tg * TG:(tg + 1) * TG], False, not diag,
                     [bconst, mbtb], [psb_])
            if diag:
                B.mm(ps_[:], identb[:], cm[:, kt - 4 * tg, :], False, True, [bconst], [psb_])
            return ps_, psb_

        LA = 2
        pend = [score(kt) for kt in range(min(LA, nkt))]
        for kt in range(nkt):
            if kt + LA < nkt:
                pend.append(score(kt + LA))
            ps_, psb_ = pend.pop(0)
            pt2, pt2b, _ = stb.next()
            B.act(pt2[:], ps_[:], AF.Exp, [psb_], [pt2b], scale=SCALE)
            B.mm(po[:], RB[:, VO + kt * 128: VO + (kt + 1) * 128], pt2[:], kt == 0, kt == nkt - 1,
                 fb(gRB, VO + kt * 128, 128) + [pt2b], [pob])
            B.mm(pd[:], onesb[:], pt2[:], kt == 0, kt == nkt - 1, [bconst, pt2b], [pdb])
        rc, rcb, _ = st32.next()
        B.S.add("dve", lambda e, rc=rc, pd=pd: e.reciprocal(rc[:], pd[:]), [pdb], [rcb])
        B.tt(RB[:, h * 2048 + tg * TG: h * 2048 + (tg + 1) * TG], po[:], rc[:], ALU.mult, [pob, rcb],
             fb(gRB, h * 2048 + tg * TG, TG))

    load_head_v(0)
    for tg in range(NTG):
        proj(0, tg)
    gate_a(0)
    gate_b(0)
    pre_b = None
    for h in range(8):
        nh = h + 1 if h + 1 < 8 else None
        if nh is not None:
            load_head_w(nh)
        else:
            pre_b = branch_ld(w_b, 7168, 0)
        for tg in range(3):
            att(h, tg)
            if nh is not None:
                proj(nh, tg)
        if nh is not None:
            proj(nh, 3)
            gate_a(nh)
        att(h, 3)
        if nh is not None:
            gate_b(nh)
    if stop == "p1d":
        dbg_dump(RB, gRB, 16384)
        return

    branch(w_b, 7168, 0, True, preloaded=pre_b)
    if stop == "p1e":
        return

    memT = yt.t[0][:].bitcast(BF16)
    gY = [[yt.b[0]] * 16, [yt.b[0]] * 16]
    cmf = cm[:].rearrange("p a b -> p (a b)")
    bKV = Buf("kv")
    KMc, VMc = 0, 1024
    for mtile in range(2):
        t, tb, td = load_tile(mem[mtile * 128:(mtile + 1) * 128, :], [])
        prenorm_T(t[:], tb, 2, memT, gY, 0, 256, mtile * 128)
    for blk in range(2):
        (wk2,), wk2b = wload([(w_ckv, 0, D, blk * 256, 256)])
        for jj in range(2):
            hh = blk * 2 + jj
            p_, pb_ = ps_next()
            for kc in range(KC):
                B.mm(p_[:, 0:256], wk2[:, kc, jj * 128:(jj + 1) * 128], memT[:, kc * 256:(kc + 1) * 256],
                     kc == 0, kc == KC - 1, [wk2b, yt.b[0]], [pb_])
            B.cp(cmf[:, KMc + hh * 256: KMc + (hh + 1) * 256], p_[:, 0:256], [pb_], [bconst, bKV], eng="act")
    for blk in range(2):
        (wv2,), wv2b = wload([(w_ckv, 0, D, 512 + blk * 256, 256)])
        for ktile in range(2):
            p_, pb_ = ps_next()
            for kc in range(KC):
                B.mm(p_[:, 0:256], memT[:, kc * 256 + ktile * 128: kc * 256 + ktile * 128 + 128], wv2[:, kc, :],
                     kc == 0, kc == KC - 1, [wv2b, yt.b[0]], [pb_])
            B.cp(cmf[:, VMc + ktile * 512 + blk * 256: VMc + ktile * 512 + (blk + 1) * 256], p_[:, 0:256], [pb_],
                 [bconst, bKV], eng="act")


    RA3 = RA[:].rearrange("p (k n) -> p k n", k=KC)
    for cb in range(4):
        dwc = B.dsem("wtm%d" % cb)
        for hf in range(2):
            cast_load(RA3[:, hf * 8:(hf + 1) * 8, cb * 512:(cb + 1) * 512], w_tm, hf * 1024, 1024, cb * 512, 512, dwc,
                      [gRA[kc][cb * 4 + s_] for kc in range(hf * 8, hf * 8 + 8) for s_ in range(4)])
    load_gtab(gains[0:1, :])
    ybufs = [(yt.t[0][:], yt.b[0]), (slots[0][:].bitcast(F32), bslot[0])]
    dy3 = [B.dsem("y3l0"), B.dsem("y3l1")]
    mts = [(slots[1][:, 0:2048], Buf("mt0")), (slots[1][:, 2048:4096], Buf("mt1"))]
    dmts = [B.dsem("mt0"), B.dsem("mt1")]
    ybanks = [(psum[i], bps[i]) for i in range(4)]
    tbanks = [[(psum[4], bps[4]), (psum[5], bps[5])], [(psum[6], bps[6]), (psum[7], bps[7])]]

    def load_mT(tt):
        t, tb = mts[tt % 2]
        B.dma("sp", t, MG[tt], dmts[tt % 2], [bMG[k][tt // 4] for k in range(KC)], [tb, bslot[1]])

    def p2a_mm(tt, res):
        if tt + 1 < NT:
            load_mT(tt + 1)
        mt_, mtb = mts[tt % 2]
        for cb in range(4):
            p, pb = ybanks[cb]
            for kc in range(KC):
                B.mm(p[:], mt_[:, kc * 128:(kc + 1) * 128], RA[:, kc * 2048 + cb * 512: kc * 2048 + (cb + 1) * 512],
                     kc == 0, kc == KC - 1, [mtb, bslot[1]] + gRA[kc][cb * 4:cb * 4 + 4], [pb])

    def sublayer_pipe(mm_fn, res_src, res_bufs, Hout, bH, gi, res_tiles, act_identity):
        st = {}

        def load_res(tt):
            t, tb, td = res_tiles[tt % len(res_tiles)]
            B.dma("sp", t[:], res_src[tt * 128:(tt + 1) * 128, :], td, res_bufs(tt), [tb])
            st[tt] = [t, tb, td]

        def S1(tt):
            y, yb = ybufs[tt % 2]
            s, sbf, _ = sm.next()
            for c, (p_, pb_) in enumerate(ybanks):
                jk, jkb, _ = stb.next()
                B.act(jk[:], p_[:], AF.Square, [pb_], [jkb, sbf], accum=s[:, c:c + 1])
                B.act(y[:, c * 512:(c + 1) * 512], p_[:], AF.Identity, [pb_], [yb])
            st[tt] += [y, yb, s, sbf]

        def CH(tt):
            t, tb, td, y, yb, s, sbf = st[tt]
            r, rb = rstd_of(s[:, 0:4], sbf, 4, D)
            B.stt(y, y, r, gtab[:], ALU.mult, ALU.mult, [yb, rb, bgt], [yb])
            B.tt(t[:], y, t[:], ALU.add, [yb, tb], [tb])
            B.dma("pool", Hout[tt * 128:(tt + 1) * 128, :], t[:], td, [tb], [bH[tt]])
            n, nbuf, _ = nb.next()
            s2, s2b, _ = sm.next()
            B.stt(n[:], t[:], 1.0, t[:], ALU.mult, ALU.mult, [tb], [nbuf, s2b], accum=s2[:, 0:1])
            r2, r2b = rstd_of(s2[:, 0:1], s2b, 1, D)
            if act_identity:
                B.act(n[:], t[:], AF.Identity, [tb, r2b], [nbuf], scale=r2)
            else:
                B.ts(n[:], t[:], r2, None, ALU.mult, None, [tb, r2b], [nbuf])
            st[tt] = (n, nbuf)

        def TR(tt):
            n, nbuf = st.pop(tt)
            transpose_out(n, nbuf, gi, RB, gRB, 0, 2048, tt * 128, banks=tbanks[tt % 2])

        nres = len(res_tiles)
        for tt in range(min(nres, NT)):
            load_res(tt)
        mm_fn(0, None)
        S1(0)
        mm_fn(1, None)
        S1(1)
        CH(0)
        if nres < NT:
            load_res(nres)
        for tt in range(NT):
            if tt + 2 < NT:
                mm_fn(tt + 2, None)
                S1(tt + 2)
            if tt + 1 < NT:
                CH(tt + 1)
                if tt + 1 + nres < NT:
                    load_res(tt + 1 + nres)
            TR(tt)

    load_mT(0)
    sublayer_pipe(p2a_mm, x, lambda tt: [], H1, bH1, 1, [(xt.t[i], xt.b[i], xt.d[i]) for i in range(2)], False)
    if stop == "p2a":
        dbg_dump(RB, gRB, 32768)
        return

    QT0, OT0, WO0 = 0, 8192, 16384
    dwo = B.dsem("wo")
    for ch in range(2):
        cast_load(RA[:, WO0:WO0 + 8192].rearrange("p (k n) -> p k n", k=4)[:, :, ch * 1024:(ch + 1) * 1024], w_co, 0, 512,
                  ch * 1024, 1024, dwo, fb(gRA, WO0, 8192))
    for blk in range(2):
        (wq2,), wq2b = wload([(w_cq, 0, D, blk * 256, 256)])
        for jj in range(2):
            hh = blk * 2 + jj
            for tg in range(NTG):
                p, pb = ps_next()
                for kc in range(KC):
                    B.mm(p[:], wq2[:, kc, jj * 128:(jj + 1) * 128], RB[:, kc * 2048 + tg * TG: kc * 2048 + (tg + 1) * TG],
                         kc == 0, kc == KC - 1, [wq2b] + gRB[kc][tg * 4:tg * 4 + 4], [pb])
                B.cp(RA[:, QT0 + hh * 2048 + tg * TG: QT0 + hh * 2048 + (tg + 1) * TG], p[:], [pb],
                     fb(gRA, QT0 + hh * 2048 + tg * TG, TG), eng="act")
    its = [(hh, tg) for hh in range(4) for tg in range(NTG)]

    def ca_scores(idx):
        hh, tg = its[idx]
        res = []
        for kt in range(2):
            bi = 4 + (idx % 2) * 2 + kt
            ps_, psb_ = psum[bi], bps[bi]
            B.mm(ps_[:], cmf[:, KMc + hh * 256 + kt * 128: KMc + hh * 256 + (kt + 1) * 128],
                 RA[:, QT0 + hh * 2048 + tg * TG: QT0 + hh * 2048 + (tg + 1) * TG], True, True,
                 [bKV] + fb(gRA, QT0 + hh * 2048 + tg * TG, TG), [psb_])
            res.append((ps_, psb_))
        return res

    nsc = ca_scores(0)
    for idx, (hh, tg) in enumerate(its):
        sc = nsc
        if idx + 1 < len(its):
            nsc = ca_scores(idx + 1)
        po, pob = psum[(idx % 2) * 2], bps[(idx % 2) * 2]
        pd, pdb = psum[(idx % 2) * 2 + 1], bps[(idx % 2) * 2 + 1]
        pts = []
        for kt in range(2):
            pt2, pt2b, _ = stb.next()
            B.act(pt2[:], sc[kt][0][:], AF.Exp, [sc[kt][1]], [pt2b], scale=SCALE)
            pts.append((pt2, pt2b))
        for kt in range(2):
            pt2, pt2b = pts[kt]
            B.mm(po[:], cmf[:, VMc + kt * 512 + hh * 128: VMc + kt * 512 + (hh + 1) * 128], pt2[:], kt == 0, kt == 1,
                 [bKV, pt2b], [pob])
            B.mm(pd[:], onesb[:], pt2[:], kt == 0, kt == 1, [bconst, pt2b], [pdb])
        rc, rcb, _ = st32.next()
        B.S.add("dve", lambda e, rc=rc, pd=pd: e.reciprocal(rc[:], pd[:]), [pdb], [rcb])
        B.tt(RA[:, OT0 + hh * 2048 + tg * TG: OT0 + hh * 2048 + (tg + 1) * TG], po[:], rc[:], ALU.mult, [pob, rcb],
             fb(gRA, OT0 + hh * 2048 + tg * TG, TG))
    load_gtab(gains[1:2, :])

    def ca_mm(tt, res):
        for cb in range(4):
            p, pb = ybanks[cb]
            for hh in range(4):
                B.mm(p[:], RA[:, OT0 + hh * 2048 + tt * 128: OT0 + hh * 2048 + tt * 128 + 128],
                     RA[:, WO0 + hh * 2048 + cb * 512: WO0 + hh * 2048 + (cb + 1) * 512], hh == 0, hh == 3,
                     fb(gRA, OT0 + hh * 2048 + tt * 128, 128) + fb(gRA, WO0 + hh * 2048 + cb * 512, 512), [pb])

    ex = [RA[:, QT0 + i * 4096: QT0 + (i + 1) * 4096].bitcast(F32) for i in range(2)]
    exb = [Buf("ex0"), Buf("ex1")]
    exd = [B.dsem("ex0"), B.dsem("ex1")]

    class _T:
        def __init__(self, ap):
            self.ap = ap

        def __getitem__(self, k):
            return self.ap
    B.S.add("dve", lambda e: e.memset(ex[0][:, 0:1], 0.0), [], fb(gRA, QT0, 8192) + [exb[0], exb[1]])
    res4 = [(xt.t[0], xt.b[0], xt.d[0]), (xt.t[1], xt.b[1], xt.d[1]), (_T(ex[0]), exb[0], exd[0]), (_T(ex[1]), exb[1], exd[1])]
    sublayer_pipe(ca_mm, H1, lambda tt: [bH1[tt]], H2, bH2, 3, res4, True)
    if stop == "p2b":
        dbg_dump(RB, gRB, 32768)
        return

    def ldf(j):
        return wload([(w_fg, 0, D, j * 128, 128), (w_fu, 0, D, j * 128, 128)])

    def load_wd_chunk(cb, r, i):
        R, g = (RA, gRA) if r == 0 else (RB, gRB)
        dstv = R[:, i * 5632:(i + 1) * 5632].rearrange("p (k n) -> p k n", k=11)
        cast_load(dstv, w_fd, i * 1408, 1408, cb * 512, 512, B.dsem("wd%d_%d" % (cb, i)), fb(g, i * 5632, 5632))

    def load_wd(cb, r):
        for i in range(4):
            load_wd_chunk(cb, r, i)
    nw = ldf(0)
    B.S.add("dve", lambda e: e.memset(RA[:, 0:2], 0.0), [], [exb[0], exb[1]] + fb(gRA, 0, 8192))
    for j in range(HC):
        (wg_, wu_), wb_ = nw
        if j + 1 < HC:
            nw = ldf(j + 1)
        if j in (2, 4, 6, 8):
            load_wd_chunk(0, 0, (j - 2) // 2)
        for tg in range(NTG):
            pg, pgb = ps_next()
            for kc in range(KC):
                B.mm(pg[:], wg_[:, kc, :], RB[:, kc * 2048 + tg * TG: kc * 2048 + (tg + 1) * TG],
                     kc == 0, kc == KC - 1, [wb_] + gRB[kc][tg * 4:tg * 4 + 4], [pgb])
            pu, pub = ps_next()
            for kc in range(KC):
                B.mm(pu[:], wu_[:, kc, :], RB[:, kc * 2048 + tg * TG: kc * 2048 + (tg + 1) * TG],
                     kc == 0, kc == KC - 1, [wb_] + gRB[kc][tg * 4:tg * 4 + 4], [pub])
            sg, sgb, _ = st32.next()
            B.act(sg[:], pg[:], AF.Silu, [pgb], [sgb])
            ob, obb, obd = stb.next()
            B.tt(ob[:], sg[:], pu[:], ALU.mult, [sgb, pub], [obb])
            B.dma("pool", HIDT[tg * 4:(tg + 1) * 4, :, j * 128:(j + 1) * 128].rearrange("c p t -> p c t"),
                  ob[:].rearrange("p (c t) -> p c t", c=4), obd, [obb], [bHID[j][tg]])
    if stop == "p3a":
        return

    HT0 = 22528
    regs = ((RA, gRA), (RB, gRB))
    dhts = [B.dsem("ht0"), B.dsem("ht1")]

    hidx = [0]

    def load_hT(tt):
        i = hidx[0] % 2
        hidx[0] += 1
        R, g = regs[i]
        B.dma("sp", R[:, HT0:HT0 + 5632], HIDT[tt], dhts[i], [bHID[k][tt // 4] for k in range(HC)], fb(g, HT0, 5632))
        return R, g

    def down_mm(p, pb, HR, hgd, WR, wgd):
        for k in range(HC):
            B.mm(p[:], HR[:, HT0 + k * 128: HT0 + (k + 1) * 128], WR[:, k * 512:(k + 1) * 512], k == 0, k == HC - 1,
                 fb(hgd, HT0 + k * 128, 128) + fb(wgd, k * 512, 512), [pb])

    load_wd(1, 1)
    seq = [(cb, tt) for cb in range(2) for tt in range(NT)] + [(2, tt) for tt in range(NT)]
    nxt_h = load_hT(seq[0][1])
    for si, (cb, tt) in enumerate(seq):
        HR, hgd = nxt_h
        if si + 1 < len(seq):
            nxt_h = load_hT(seq[si + 1][1])
        if cb < 2:
            if cb == 1 and tt == 0:
                load_wd(2, 0)
            WR, wgd = regs[cb]
            p, pb = ps_next()
            down_mm(p, pb, HR, hgd, WR, wgd)
            s_, sb_, sd_ = st32.next()
            B.cp(s_[:], p[:], [pb], [sb_], eng="act")
            B.dma("pool", Y3[tt * 128:(tt + 1) * 128, cb * 512:(cb + 1) * 512], s_[:], sd_, [sb_], [bY3[tt]])
        else:
            if tt == 0:
                load_wd(3, 1)
                load_gtab(gains[2:3, :])
            ht_, htb, htd = load_tile(H2[tt * 128:(tt + 1) * 128, :], [bH2[tt]])
            y, yb = ybufs[tt % 2]
            B.dma("sp", y[:, 0:1024], Y3[tt * 128:(tt + 1) * 128, :], dy3[tt % 2], [bY3[tt]], [yb])
            pa = ps_next()
            down_mm(pa[0], pa[1], HR, hgd, RA, gRA)
            pb2 = ps_next()
            down_mm(pb2[0], pb2[1], HR, hgd, RB, gRB)
            n, nbuf, _ = nb.next()
            s, sbf = post_evac([pa, pb2], y, yb, n, nbuf, col0=1024)
            B.act(n[:, 1024:2048], y[:, 0:1024], AF.Square, [yb], [nbuf, sbf], accum=s[:, 2:3])
            post_apply(s, sbf, 3, y, yb, ht_[:], htb)
            B.dma("pool", out[tt * 128:(tt + 1) * 128, :], ht_[:], htd, [htb], [Buf("o")])
def _consts():
    bf = ml_dtypes.bfloat16
    c = {}
    c["c_identb"] = np.eye(128, dtype=np.float32).astype(bf)
    c["c_identf"] = np.eye(128, dtype=np.float32)
    c["c_onesb"] = np.ones((128, 128), dtype=np.float32).astype(bf)
    s = np.arange(128)
    c["c_tri"] = (s[:, None] <= s[None, :]).astype(np.float32)
    k = np.arange(128)[:, None, None]
    j = np.arange(4)[None, :, None]
    q = np.arange(512)[None, None, :]
    c["c_cm"] = np.where((j * 128 + k) > q, NEG, 0.0).astype(np.float32).astype(bf).reshape(128, 2048)
    e = np.zeros((8, 8, 128), np.float32)
    for n in range(8):
        e[n, n, :] = 1.0
    c["c_eall"] = e.astype(bf).reshape(8, 1024)
    qt = np.arange(16)[:, None]
    m = np.arange(8)[None, :]
    own = qt // 2
    ib = np.where(m >= own, -1e30, 0.0).astype(np.float32)
    nv = np.where(m < own, NEG, 0.0).astype(np.float32)
    c["c_ib"] = np.broadcast_to(ib.reshape(1, 128), (128, 128)).copy()
    c["c_nv"] = np.broadcast_to(nv.reshape(1, 128), (128, 128)).copy()
    col = np.zeros((128, 4), np.float32)
    inv_freq = (1.0 / (np.float32(500000.0) ** (np.arange(0, 32, 2, dtype=np.float32) / np.float32(32)))).astype(np.float32)
    inv64 = 1.0 / (500000.0 ** (np.arange(0, 32, 2, dtype=np.float64) / 32.0))
    inv_hi = inv64.astype(np.float32)
    inv_lo = (inv64 - inv_hi.astype(np.float64)).astype(np.float32)
    col[0:16, 0] = inv_hi
    col[16:32, 0] = inv_hi
    col[0:16, 2] = inv_lo
    col[16:32, 2] = inv_lo
    col[0:16, 1] = -1.0
    col[16:32, 1] = 1.0
    c["c_col"] = col
    pm = np.zeros((128, 32), np.float32)
    for m in range(16):
        pm[m + 16, m] = -1.0
        pm[m, m + 16] = 1.0
    c["c_perm"] = pm
    return c


_NC_CACHE = {}


def _get_nc(dbg=False, stop=None):
    key = (dbg, stop)
    if key not in _NC_CACHE:
        _NC_CACHE[key] = build_program(dbg, stop)
    return _NC_CACHE[key]


def make_in_maps(inputs):
    f = lambda a: np.ascontiguousarray(np.asarray(a))
    c = _consts()
    gains = np.stack([f(inputs[k])[0] for k in ("tm_post_g", "ca_post_g", "ffn_post_g")]).astype(np.float32)
    gcols = np.concatenate([f(inputs[k])[0].reshape(16, 128).T for k in ("tm_pre_g", "ca_pre_g", "mem_norm_g",
                                                                          "ffn_pre_g")], axis=1).astype(np.float32)
    gcols = np.ascontiguousarray(gcols)
    lnv = np.concatenate([f(inputs["ln_v_g"])[0], f(inputs["ln_v_b"])[0]]).reshape(1, 2048).astype(np.float32)
    shared = {
        "w_in": f(inputs["w_in"])[0], "w_sp": f(inputs["w_spatial"])[0],
        "b_sp": f(inputs["b_spatial"])[0].reshape(1, 1024),
        "w_a": f(inputs["w_branch_a"])[0], "w_b": f(inputs["w_branch_b"])[0], "w_tm": f(inputs["w_tm_out"])[0],
        "w_cq": f(inputs["w_ca_q"])[0], "w_ckv": f(inputs["w_ca_kv"])[0], "w_co": f(inputs["w_ca_out"])[0],
        "w_fg": f(inputs["w_ffn_gate"])[0], "w_fu": f(inputs["w_ffn_up"])[0], "w_fd": f(inputs["w_ffn_down"])[0],
        "gains": gains, "lnv": lnv, "gcols": gcols,
    }
    shared.update(c)
    xs = f(inputs["x"])
    mems = f(inputs["mem"])
    ps = f(inputs["positions"]).astype(np.int32)
    maps = []
    for b in range(8):
        m = dict(shared)
        m["x"] = xs[b]
        m["mem"] = mems[b]
        m["pos"] = ps[b].reshape(1, SEQ)
        maps.append(m)
    return maps


def kernel(**inputs):
    nc = _get_nc()
    maps = make_in_maps(inputs)
    res = run_bass_kernel_spmd(nc, maps, core_ids=list(range(8)))
    return np.stack([np.asarray(r["out"]) for r in res.results], axis=0).astype(np.float32)
```

```python
import os
from contextlib import ExitStack

import numpy as np
import ml_dtypes

import concourse.bass as bass
import concourse.mybir as mybir
from concourse.bass_utils import run_bass_kernel_spmd

F32 = mybir.dt.float32
BF16 = mybir.dt.bfloat16
I32 = mybir.dt.int32
ALU = mybir.AluOpType
AF = mybir.ActivationFunctionType
AX = mybir.AxisListType

D = 2048
SEQ = 2048
NT = 16
TG = 512
NTG = 4
KC = 16
MEM = 256
GW = 1024
HID = 5632
HC = 44
EPS = 1e-6
NEG = -30000.0
SEM_LIMIT = 24000
STRICT_SYNC = True


class Buf:
    __slots__ = ("name", "last_w", "rd_eng", "rd_dma", "excl")

    def __init__(self, name, excl=False):
        self.name = name
        self.excl = excl
        self.last_w = None
        self.rd_eng = {}
        self.rd_dma = []


class DSem:
    def __init__(self, sched, name):
        self.s = sched
        self.name = name
        self.sem = None
        self.count = 0
        self.gen = 0
        self.twin = None
        sched.dsems.append(self)

    def bump(self):
        if self.sem is None:
            self.sem = self.s.new_sem(self.name)
        if self.count + 16 > SEM_LIMIT:
            self.s.final_list.append((self.sem, self.count))
            self.gen += 1
            self.sem = self.s.new_sem("%s_g%d" % (self.name, self.gen))
            self.count = 0
        self.count += 16
        return self.sem, self.count


class Op:
    __slots__ = ("eng", "fn", "waits", "dma", "need_inc", "inc", "idx", "dsem", "dval", "dsem_obj")


ENGS = ("pe", "act", "dve", "pool", "sp")


class Sched:
    def __init__(self, nc, stack):
        self.nc = nc
        self.stack = stack
        self.ops = {e: [] for e in ENGS}
        self.nsem = 0
        self.dsems = []
        self.final_list = []

    def new_sem(self, name):
        self.nsem += 1
        return self.stack.enter_context(self.nc.semaphore("s%d_%s" % (self.nsem, name)))

    def _dep(self, op, p, raw):
        if p is None or p is op:
            return
        if p.dma:
            so = p.dsem_obj
            if so.sem is p.dsem:
                op.waits.append(("d", p.dsem, so.count))
            else:
                op.waits.append(("d", p.dsem, p.dval))
            return
        if p.eng == op.eng and not op.dma:
            if p.eng == "pe":
                return
            if not raw and not STRICT_SYNC:
                return
        p.need_inc = True
        op.waits.append(("c", p))

    def add(self, eng, fn, reads=(), writes=(), dma=False, dsem=None):
        op = Op()
        op.eng = eng
        op.fn = fn
        op.waits = []
        op.dma = dma
        op.need_inc = False
        op.inc = None
        op.idx = len(self.ops[eng])
        op.dsem = None
        op.dval = 0
        op.dsem_obj = dsem
        for b in reads:
            self._dep(op, b.last_w, True)
            if b.excl:
                for r in b.rd_eng.values():
                    if r.eng != eng:
                        self._dep(op, r, False)
        for b in writes:
            lw = b.last_w
            if not (dma and lw is not None and lw.dma and lw.dsem_obj is dsem):
                self._dep(op, lw, False)
            for r in b.rd_eng.values():
                self._dep(op, r, False)
            for r in b.rd_dma:
                self._dep(op, r, False)
        for b in writes:
            b.last_w = op
            b.rd_eng = {}
            b.rd_dma = []
        for b in reads:
            if dma:
                b.rd_dma.append(op)
                if len(b.rd_dma) > 8:
                    b.rd_dma = b.rd_dma[-8:]
            else:
                b.rd_eng[eng] = op
        if dma:
            op.dsem, op.dval = dsem.bump()
        self.ops[eng].append(op)
        return op

    def emit(self):
        nc = self.nc
        esem = {}
        ecount = {}
        for e in ENGS:
            esem[e] = self.new_sem("eng_" + e)
            ecount[e] = 0
            for op in self.ops[e]:
                if op.dma or not op.need_inc:
                    continue
                if ecount[e] + 1 > SEM_LIMIT:
                    esem[e] = self.new_sem("eng_" + e)
                    ecount[e] = 0
                ecount[e] += 1
                op.inc = (esem[e], ecount[e])
        fin = list(self.final_list) + [(d.sem, d.count) for d in self.dsems if d.sem is not None]

        def run(e, engine, last=False):
            waited = {}
            for op in self.ops[e]:
                for w in op.waits:
                    if w[0] == "d":
                        sem, val = w[1], w[2]
                    else:
                        sem, val = w[1].inc
                    k = id(sem)
                    if waited.get(k, 0) >= val:
                        continue
                    waited[k] = val
                    engine.wait_ge(sem, val)
                ins = op.fn(engine)
                if op.dma:
                    ins.then_inc(op.dsem, 16)
                elif op.inc is not None:
                    ins.then_inc(op.inc[0], 1)
            if last:
                for (fs, fc) in fin:
                    engine.wait_ge(fs, fc)

        with nc.Block() as block:
            @block.tensor
            def _(eng):
                run("pe", eng)

            @block.scalar
            def _(eng):
                run("act", eng)

            @block.vector
            def _(eng):
                run("dve", eng)

            @block.gpsimd
            def _(eng):
                run("pool", eng)

            @block.sync
            def _(eng):
                run("sp", eng, last=True)


class Builder:
    def __init__(self, nc, stack, dbg=False, stop=None):
        self.nc = nc
        self.st = stack
        self.S = Sched(nc, stack)
        self.dbg = dbg
        self.stop = stop
        self.ps_i = 0

    def sb(self, name, shape, dt):
        return self.st.enter_context(self.nc.sbuf_tensor(name, list(shape), dt))

    def dram_in(self, name, shape, dt):
        return self.nc.dram_tensor(name, list(shape), dt, kind="ExternalInput").ap()

    def dram_out(self, name, shape, dt):
        return self.nc.dram_tensor(name, list(shape), dt, kind="ExternalOutput").ap()

    def scratch(self, name, shape, dt):
        kind = "ExternalOutput" if self.dbg else "Internal"
        return self.nc.dram_tensor(name, list(shape), dt, kind=kind).ap()

    def mm(self, out, lhsT, rhs, start, stop, reads, writes):
        self.S.add("pe", lambda e: e.matmul(out, lhsT, rhs, start=start, stop=stop), reads, writes)

    def tr(self, out, in_, ident, reads, writes):
        self.S.add("pe", lambda e: e.transpose(out, in_, ident), reads, writes)

    def act(self, out, in_, func, reads, writes, scale=None, bias=None, accum=None):
        kw = {}
        if scale is not None:
            kw["scale"] = scale
        if bias is not None:
            kw["bias"] = bias
        if accum is not None:
            kw["accum_out"] = accum
        self.S.add("act", lambda e: e.activation(out, in_, func, **kw), reads, writes)

    def tt(self, out, in0, in1, op, reads, writes, eng="dve"):
        self.S.add(eng, lambda e: e.tensor_tensor(out, in0, in1, op), reads, writes)

    def ts(self, out, in0, s1, s2, op0, op1, reads, writes, eng="dve"):
        if op1 is None:
            self.S.add(eng, lambda e: e.tensor_scalar(out, in0, s1, None, op0), reads, writes)
        else:
            self.S.add(eng, lambda e: e.tensor_scalar(out, in0, s1, s2, op0, op1), reads, writes)

    def stt(self, out, in0, scalar, in1, op0, op1, reads, writes, accum=None):
        if accum is None:
            self.S.add("dve", lambda e: e.scalar_tensor_tensor(out, in0, scalar, in1, op0, op1), reads, writes)
        else:
            self.S.add("dve", lambda e: e.scalar_tensor_tensor(out, in0, scalar, in1, op0, op1, accum_out=accum),
                       reads, writes)

    def cp(self, out, in_, reads, writes, eng="dve"):
        if eng == "act":
            self.S.add("act", lambda e: e.copy(out, in_), reads, writes)
        else:
            self.S.add(eng, lambda e: e.tensor_copy(out, in_), reads, writes)

    def dma(self, q, out, in_, dsem, reads, writes):
        if q != "sp":
            if dsem.twin is None:
                dsem.twin = DSem(self.S, dsem.name + "_sw")
            dsem = dsem.twin
        return self.S.add(q, lambda e: e.dma_start(out=out, in_=in_), reads, writes, dma=True, dsem=dsem)

    def dsem(self, name):
        return DSem(self.S, name)


def build_program(dbg=False, stop=None):
    nc = bass.Bass("TRN2", target_bir_lowering=False)
    with ExitStack() as st:
        B = Builder(nc, st, dbg, stop)
        _program(B)
        B.S.emit()
    return nc


PHASES = ("p0", "p1a", "p1b", "p1c", "p1d", "p1e", "p2a", "p2b", "p3a", "p3b", "p3c")


def _program(B):
    nc = B.nc
    S = B.S
    stop = B.stop
    PI = float(np.pi)

    x = B.dram_in("x", [SEQ, D], F32)
    mem = B.dram_in("mem", [MEM, D], F32)
    pos = B.dram_in("pos", [1, SEQ], I32)
    w_in = B.dram_in("w_in", [D, 9216], F32)
    w_sp = B.dram_in("w_sp", [8, 128, 128], F32)
    b_sp = B.dram_in("b_sp", [1, 1024], F32)
    w_a = B.dram_in("w_a", [GW, D], F32)
    w_b = B.dram_in("w_b", [GW, D], F32)
    w_tm = B.dram_in("w_tm", [D, D], F32)
    w_cq = B.dram_in("w_cq", [D, 512], F32)
    w_ckv = B.dram_in("w_ckv", [D, 1024], F32)
    w_co = B.dram_in("w_co", [512, D], F32)
    w_fg = B.dram_in("w_fg", [D, HID], F32)
    w_fu = B.dram_in("w_fu", [D, HID], F32)
    w_fd = B.dram_in("w_fd", [HID, D], F32)
    gains = B.dram_in("gains", [3, D], F32)
    gcols_d = B.dram_in("gcols", [128, 64], F32)
    lnv = B.dram_in("lnv", [1, 2 * GW], F32)
    c_identb = B.dram_in("c_identb", [128, 128], BF16)
    c_identf = B.dram_in("c_identf", [128, 128], F32)
    c_onesb = B.dram_in("c_onesb", [128, 128], BF16)
    c_tri = B.dram_in("c_tri", [128, 128], F32)
    c_cm = B.dram_in("c_cm", [128, 4 * 512], BF16)
    c_eall = B.dram_in("c_eall", [8, 8 * 128], BF16)
    c_ib = B.dram_in("c_ib", [128, 128], F32)
    c_nv = B.dram_in("c_nv", [128, 128], F32)
    c_col = B.dram_in("c_col", [128, 4], F32)
    c_perm = B.dram_in("c_perm", [128, 32], F32)
    out = B.dram_out("out", [SEQ, D], F32)

    CS = B.scratch("CS", [2, 128, SEQ], F32)
    MA = B.scratch("MA", [KC, 128, SEQ], F32)
    MG = B.scratch("MG", [NT, 128, KC * 128], BF16)
    H1 = B.scratch("H1", [SEQ, D], F32)
    H2 = B.scratch("H2", [SEQ, D], F32)
    HIDT = B.scratch("HIDT", [NT, 128, HC * 128], BF16)
    Y3 = B.scratch("Y3", [SEQ, 1024], F32)
    DBG = B.dram_out("DBG", [128, 32768], F32) if B.dbg else None

    bCS = Buf("CS")
    bMA = [[Buf("MA") for _ in range(NTG)] for _ in range(KC)]
    bMG = [[Buf("MG") for _ in range(NTG)] for _ in range(KC)]
    bH1 = [Buf("H1") for _ in range(NT)]
    bH2 = [Buf("H2") for _ in range(NT)]
    bHID = [[Buf("HID") for _ in range(NTG)] for _ in range(HC)]
    bY3 = [Buf("Y3") for _ in range(NT)]

    RA = B.sb("RA", [128, 32768], BF16)
    RB = B.sb("RB", [128, 32768], BF16)
    gRA = [[Buf("RA") for _ in range(16)] for _ in range(16)]
    gRB = [[Buf("RB") for _ in range(16)] for _ in range(16)]

    def fb(grid, start, length):
        res = []
        s0 = start // 128
        s1 = (start + length - 1) // 128
        for s in range(s0, s1 + 1):
            res.append(grid[s // 16][s % 16])
        return res

    def fb2(grid, start, nrows, rstride, length):
        res = []
        for r in range(nrows):
            res.extend(fb(grid, start + r * rstride, length))
        return res

    NSLOT = 2
    SLOTN = 4096
    slots = [B.sb("wslot%d" % i, [128, SLOTN], BF16) for i in range(NSLOT)]
    bslot = [Buf("wslot%d" % i) for i in range(NSLOT)]
    dslot = [B.dsem("wslot%d" % i) for i in range(NSLOT)]
    slot_i = [0]

    identb = B.sb("identb", [128, 128], BF16)
    identf = B.sb("identf", [128, 128], F32)
    onesb = B.sb("onesb", [128, 128], BF16)
    tri = B.sb("tri", [128, 128], F32)
    cm = B.sb("cm", [128, 4, 512], BF16)
    eall = B.sb("eall", [8, 8, 128], BF16)
    ibt = B.sb("ibt", [128, 16, 8], F32)
    nvt = B.sb("nvt", [128, 16, 8], F32)
    ccol = B.sb("ccol", [128, 4], F32)
    permf = B.sb("permf", [128, 32], F32)
    mhalf = B.sb("mhalf", [128, 1], F32)
    itile = B.sb("itile", [128, 256], I32)
    bit = Buf("itile")
    dit = B.dsem("itile")
    bmh = Buf("mhalf")
    B.S.add("pool", lambda e: e.memset(mhalf[:], -0.5), [], [bmh])
    gcols = B.sb("gcols_sb", [128, 4, 16], F32)
    gtab = B.sb("gtab", [128, D], F32)
    bgt = Buf("gtab")
    bconst = Buf("const")
    dconst = B.dsem("const")
    for (dst, src) in ((identb[:], c_identb), (identf[:], c_identf), (onesb[:], c_onesb), (tri[:], c_tri),
                       (cm[:].rearrange("p a b -> p (a b)"), c_cm),
                       (eall[:].rearrange("p a b -> p (a b)"), c_eall),
                       (ibt[:].rearrange("p a b -> p (a b)"), c_ib),
                       (nvt[:].rearrange("p a b -> p (a b)"), c_nv), (ccol[:], c_col), (permf[:], c_perm),
                       (gcols[:].rearrange("p a b -> p (a b)"), gcols_d)):
        B.dma("sp", dst, src, dconst, [], [bconst])

    def load_gtab(src_ap):
        B.dma("sp", gtab[:], src_ap.partition_broadcast(128), dconst, [], [bgt])

    psum = [B.st.enter_context(nc.psum_tensor("ps%d" % i, [128, 512], F32)) for i in range(8)]
    bps = [Buf("ps%d" % i, excl=True) for i in range(8)]

    def ps_next():
        i = B.ps_i % 8
        B.ps_i += 1
        return psum[i], bps[i]

    class Rot:
        def __init__(self, name, shape, dt, n):
            self.t = [B.sb("%s%d" % (name, i), shape, dt) for i in range(n)]
            self.b = [Buf("%s%d" % (name, i)) for i in range(n)]
            self.d = [B.dsem("%s%d" % (name, i)) for i in range(n)]
            self.i = 0

        def next(self):
            i = self.i % len(self.t)
            self.i += 1
            return self.t[i], self.b[i], self.d[i]

    xt = Rot("xt", [128, D], F32, 2)
    yt = Rot("yt", [128, D], F32, 1)
    nb = Rot("nb", [128, D], BF16, 2)
    st32 = Rot("st32", [128, TG], F32, 3)
    stb = Rot("stb", [128, TG], BF16, 3)
    sm = Rot("sm", [128, 16], F32, 8)
    mb = Rot("mb", [128, 128], BF16, 2)
    gmt = Rot("gmt", [128, 128], F32, 2)
    cmpt = yt.t[0][:, 0:1024]
    bcmp = yt.b[0]
    mbt = nb.t[0]
    bmbt = [nb.b[0] for _ in range(NTG)]
    kmt = B.sb("kmt", [128, 16], BF16)
    bkmt = Buf("kmt")
    wspT = B.sb("wspT", [128, 8, 128], BF16)
    bwsp = Buf("wspT")
    bsr = nb.t[1][0:1, :].rearrange("p (a b) -> p a b", a=2)
    bbsr = nb.b[1]

    def wload_into(i, off, W, r0, nrows, c0, ncols):
        kc = nrows // 128
        v = slots[i][:, off:off + kc * ncols].rearrange("p (k n) -> p k n", k=kc)
        src = W[r0:r0 + nrows, c0:c0 + ncols].rearrange("(k p) n -> p k n", p=128)
        B.dma("pool", v, src, dslot[i], [], [bslot[i]])
        return v

    def wload(specs):
        i = slot_i[0] % NSLOT
        slot_i[0] += 1
        off = 0
        views = []
        for (W, r0, nrows, c0, ncols) in specs:
            views.append(wload_into(i, off, W, r0, nrows, c0, ncols))
            off += (nrows // 128) * ncols
        assert off <= SLOTN
        return views, bslot[i]

    def cast_load(dst_ap, W, r0, nrows, c0, ncols, dsem, wbufs):
        src = W[r0:r0 + nrows, c0:c0 + ncols].rearrange("(k p) n -> p k n", p=128)
        B.dma("pool", dst_ap, src, dsem, [], wbufs)

    def dbg_dump(region, grid, nel):
        for c in range(nel // TG):
            s_, sb_, sd = st32.next()
            B.cp(s_[:], region[:, c * TG:(c + 1) * TG], fb(grid, c * TG, TG), [sb_])
            B.dma("sp", DBG[:, c * TG:(c + 1) * TG], s_[:], sd, [sb_], [Buf("d")])

    def rstd_of(ss_ap, ss_buf, ncols, dim):
        t, b, _ = sm.next()
        if ncols > 1:
            B.S.add("dve", lambda e: e.tensor_reduce(t[:, 0:1], ss_ap, AX.X, ALU.add), [ss_buf], [b])
            B.ts(t[:, 1:2], t[:, 0:1], 1.0 / dim, EPS, ALU.mult, ALU.add, [b], [b])
        else:
            B.ts(t[:, 1:2], ss_ap, 1.0 / dim, EPS, ALU.mult, ALU.add, [ss_buf], [b])
        B.tt(t[:, 2:3], t[:, 1:2], mhalf[:, 0:1], ALU.pow, [b, bmh], [b], eng="pool")
        return t[:, 2:3], b

    def prenorm_stats(h_ap, h_buf, n, nbuf, dve_sq=False):
        s, sbf, _ = sm.next()
        if dve_sq:
            B.stt(n[:], h_ap, 1.0, h_ap, ALU.mult, ALU.mult, [h_buf], [nbuf, sbf], accum=s[:, 0:1])
        else:
            B.act(n[:], h_ap, AF.Square, [h_buf], [nbuf, sbf], accum=s[:, 0:1])
        r, rb = rstd_of(s[:, 0:1], sbf, 1, D)
        B.act(n[:], h_ap, AF.Identity, [h_buf, rb], [nbuf], scale=r)

    def transpose_out(n, nbuf, gi, dst, dgrid, dst_off, dst_rstride, tok0, banks=None):
        for half in range(2):
            p, pb = ps_next() if banks is None else banks[half]
            pv = p[:].bitcast(BF16)
            for j in range(8):
                kc = half * 8 + j
                B.tr(pv[:, j * 128:(j + 1) * 128], n[:, kc * 128:(kc + 1) * 128], identb[:], [nbuf, bconst], [pb])
            o = dst[:, dst_off + half * 8 * dst_rstride: dst_off + (half + 1) * 8 * dst_rstride]
            o = o.rearrange("p (k t) -> p k t", k=8)[:, :, tok0:tok0 + 128]
            wb = fb2(dgrid, dst_off + half * 8 * dst_rstride + tok0, 8, dst_rstride, 128)
            g = gcols[:, gi, half * 8:half * 8 + 8].unsqueeze(2).to_broadcast([128, 8, 128])
            B.tt(o, pv.rearrange("p (k t) -> p k t", k=8), g, ALU.mult, [pb, bconst], wb)

    def prenorm_T(h_ap, h_buf, gi, dst, dgrid, dst_off, dst_rstride, tok0):
        n, nbuf, _ = nb.next()
        prenorm_stats(h_ap, h_buf, n, nbuf)
        transpose_out(n, nbuf, gi, dst, dgrid, dst_off, dst_rstride, tok0)

    def post_evac(pss, y, yb, junk, junkb, col0=0):
        s, sbf, _ = sm.next()
        for c, (p, pb) in enumerate(pss):
            B.act(junk[:, c * 512:(c + 1) * 512], p[:], AF.Square, [pb], [junkb, sbf], accum=s[:, c:c + 1])
            B.cp(y[:, col0 + c * 512: col0 + (c + 1) * 512], p[:], [pb], [yb])
        return s, sbf

    def post_apply(s, sbf, ncols, y, yb, res_ap, res_buf):
        r, rb = rstd_of(s[:, 0:ncols], sbf, ncols, D)
        B.stt(y, y, r, gtab[:], ALU.mult, ALU.mult, [yb, rb, bgt], [yb])
        B.tt(res_ap, y, res_ap, ALU.add, [yb, res_buf], [res_buf])

    rope_ops = []

    class _Defer:
        def __getattr__(self, name):
            def f(*args, **kw):
                rope_ops.append(lambda: getattr(B, name)(*args, **kw))
            return f
    Q = _Defer()

    def rope_tables():
        a, ab, ad = xt.next()
        c_, cb_, cd = xt.next()
        w_, wb_, _ = yt.next()
        for ch in range(8):
            sl = slice(ch * 256, (ch + 1) * 256)
            Q.dma("sp", itile[:], pos[:, sl].partition_broadcast(128), dit, [], [bit])
            Q.cp(c_[:, sl], itile[:], [bit], [cb_])
        Q.ts(w_[:], c_[:], ccol[:, 2:3], None, ALU.mult, None, [cb_, bconst], [wb_])
        Q.stt(c_[:], c_[:], ccol[:, 0:1], w_[:], ALU.mult, ALU.add, [cb_, bconst, wb_], [cb_])

        def sin_of(shift, sign_col, dst_row, dsem_):
            Q.ts(w_[:], c_[:], shift, 1.0 / (2 * PI), ALU.add, ALU.mult, [cb_], [wb_])
            for ch in range(8):
                sl = slice(ch * 256, (ch + 1) * 256)
                Q.cp(itile[:], w_[:, sl], [wb_], [bit])
                Q.cp(w_[:, sl], itile[:], [bit], [wb_])
            Q.ts(a[:], c_[:], shift, None, ALU.add, None, [cb_], [ab])
            Q.stt(a[:], w_[:], -2 * PI, a[:], ALU.mult, ALU.add, [wb_, ab], [ab])
            Q.ts(w_[:], a[:], PI, -2 * PI, ALU.is_gt, ALU.mult, [ab], [wb_])
            Q.tt(a[:], a[:], w_[:], ALU.add, [ab, wb_], [ab])
            Q.ts(a[:], a[:], -PI, PI, ALU.max, ALU.min, [ab], [ab])
            Q.act(a[:], a[:], AF.Sin, [ab], [ab])
            if sign_col is not None:
                Q.ts(a[:], a[:], ccol[:, sign_col:sign_col + 1], None, ALU.mult, None, [ab, bconst], [ab])
            Q.dma("sp", CS[dst_row], a[:], dsem_, [ab], [bCS])

        sin_of(0.0, None, 1, ad)
        sin_of(0.5 * PI, None, 0, ad)


    def load_tile(src_ap, rbufs):
        t, tb, td = xt.next()
        B.dma("sp", t[:], src_ap, td, rbufs, [tb])
        return t, tb, td

    load_gtab(lnv)
    WV0 = 8 * 2048
    WV3 = RB[:, WV0:WV0 + 16384].rearrange("p (k n) -> p k n", k=KC)
    for hf in range(2):
        for kh in range(2):
            cast_load(WV3[:, kh * 8:(kh + 1) * 8, hf * 512:(hf + 1) * 512], w_in, kh * 1024, 1024, 1024 + hf * 512, 512,
                      B.dsem("wv%d%d" % (hf, kh)),
                      [b_ for kc in range(kh * 8, kh * 8 + 8) for b_ in fb(gRB, WV0 + kc * 1024 + hf * 512, 512)])
    wl, wlb, wld = xt.next()
    wv3 = wl[:, 0:1024].rearrange("p (g s) -> p g s", g=8)
    B.dma("sp", wv3, w_sp.rearrange("g t s -> t g s"), wld, [], [wlb])
    for half in range(2):
        p, pb = ps_next()
        for j in range(4):
            g = half * 4 + j
            B.tr(p[:, j * 128:(j + 1) * 128], wv3[:, g, :], identf[:], [wlb, bconst], [pb])
        B.tt(wspT[:, half * 4:half * 4 + 4, :], p[:].rearrange("p (g t) -> p g t", g=4),
             tri[:].unsqueeze(1).to_broadcast([128, 4, 128]), ALU.mult, [pb, bconst], [bwsp])
    ybufs0 = [(yt.t[0], yt.b[0]), (slots[0][:].bitcast(F32), bslot[0])]

    p1a_ps = {}

    def p1a_pe(tt):
            pss = [ps_next(), ps_next()]
            p1a_ps[tt] = pss
            for hf in range(2):
                p, pb = pss[hf]
                for kc in range(KC):
                    B.mm(p[:], RA[:, kc * 2048 + tt * 128: kc * 2048 + tt * 128 + 128],
                         RB[:, WV0 + kc * 1024 + hf * 512: WV0 + kc * 1024 + hf * 512 + 512],
                         kc == 0, kc == KC - 1,
                         [gRA[kc][tt]] + fb(gRB, WV0 + kc * 1024 + hf * 512, 512), [pb])

    def p1a_epi(tt):
            pss = p1a_ps.pop(tt)
            y, yb = ybufs0[tt % 2]
            s, sbf, _ = sm.next()
            for hf in range(2):
                p, pb = pss[hf]
                B.act(y[:, hf * 512:(hf + 1) * 512], p[:], AF.Gelu, [pb], [yb, sbf], accum=s[:, hf:hf + 1])
            B.act(y[:, 1024:2048], y[:, 0:1024], AF.Square, [yb], [yb, sbf], accum=s[:, 2:3])
            B.tt(s[:, 3:4], s[:, 0:1], s[:, 1:2], ALU.add, [sbf], [sbf])
            B.ts(s[:, 4:5], s[:, 3:4], 1.0 / GW, None, ALU.mult, None, [sbf], [sbf])
            B.tt(s[:, 5:6], s[:, 4:5], s[:, 4:5], ALU.mult, [sbf], [sbf])
            B.stt(s[:, 6:7], s[:, 2:3], 1.0 / GW, s[:, 5:6], ALU.mult, ALU.subtract, [sbf], [sbf])
            B.ts(s[:, 9:10], s[:, 6:7], EPS, None, ALU.add, None, [sbf], [sbf])
            B.tt(s[:, 7:8], s[:, 9:10], mhalf[:, 0:1], ALU.pow, [sbf, bmh], [sbf], eng="pool")
            B.stt(s[:, 8:9], s[:, 4:5], -1.0, s[:, 7:8], ALU.mult, ALU.mult, [sbf], [sbf])
            p1a_ps[("s", tt)] = (y, yb, s, sbf)

    def p1a_apply(tt):
            y, yb, s, sbf = p1a_ps.pop(("s", tt))
            B.act(y[:, 1024:2048], y[:, 0:1024], AF.Identity, [yb, sbf], [yb], scale=s[:, 7:8], bias=s[:, 8:9])
            B.tt(y[:, 1024:2048], y[:, 1024:2048], gtab[:, 0:1024], ALU.mult, [yb, bgt], [yb])
            B.tt(RB[:, tt * 1024:(tt + 1) * 1024], y[:, 1024:2048], gtab[:, 1024:2048], ALU.add, [yb, bgt],
                 fb(gRB, tt * 1024, 1024))

    xs = {}

    def p0_load(tt):
        i = tt % 2
        B.dma("sp", xt.t[i][:], x[tt * 128:(tt + 1) * 128, :], xt.d[i], [], [xt.b[i]])

    def p0_sq(tt):
        i = tt % 2
        n, nbuf, _ = nb.next()
        s, sbf, _ = sm.next()
        B.act(n[:], xt.t[i][:], AF.Square, [xt.b[i]], [nbuf, sbf], accum=s[:, 0:1])
        xs[tt] = (n, nbuf, rstd_of(s[:, 0:1], sbf, 1, D))

    def p0_id(tt):
        i = tt % 2
        n, nbuf, (r, rb) = xs[tt]
        B.act(n[:], xt.t[i][:], AF.Identity, [xt.b[i], rb], [nbuf], scale=r)
        if tt + 2 < NT:
            p0_load(tt + 2)

    def p0_tr(tt):
        n, nbuf, _ = xs.pop(tt)
        transpose_out(n, nbuf, 0, RA, gRA, 0, 2048, tt * 128)

    p0_load(0)
    p0_load(1)
    for t0_ in range(2):
        p0_sq(t0_)
        p0_id(t0_)
        p0_tr(t0_)
    p0_sq(2)
    for tt in range(NT):
        p1a_pe(tt)
        if tt + 2 < NT:
            p0_id(tt + 2)
            p0_tr(tt + 2)
        p1a_epi(tt)
        if tt + 3 < NT:
            p0_sq(tt + 3)
        p1a_apply(tt)
    bl, blb, bld = st32.next()
    bl2, bl2b, _ = st32.next()
    B.dma("sp", bl[0:1, 0:512], b_sp[:, 0:512], bld, [], [blb])
    B.dma("sp", bl2[0:1, 0:512], b_sp[:, 512:1024], bld, [], [bl2b])
    for hh, (t_, tb_) in enumerate(((bl, blb), (bl2, bl2b))):
        B.cp(bsr[0:1, 0, hh * 512:(hh + 1) * 512], t_[0:1, 0:512], [tb_], [bbsr])
        B.tt(t_[0:1, 0:512], t_[0:1, 0:512], bsr[0:1, 0, hh * 512:(hh + 1) * 512], ALU.subtract, [tb_, bbsr], [tb_])
        B.cp(bsr[0:1, 1, hh * 512:(hh + 1) * 512], t_[0:1, 0:512], [tb_], [bbsr])

    if stop == "p1a":
        dbg_dump(RB, gRB, 16384)
        return

    AT0 = 8 * 2048

    def wu_load(blk):
        return wload([(w_in, 0, D, blk * 256, 256)])
    nw = wu_load(0)
    for blk in range(4):
        (wv_,), wb_ = nw
        if blk + 1 < 4:
            nw = wu_load(blk + 1)
        for tg in range(NTG):
            for jj in range(2):
                g = blk * 2 + jj
                pu, pub = ps_next()
                for kc in range(KC):
                    B.mm(pu[:], wv_[:, kc, jj * 128:(jj + 1) * 128], RA[:, kc * 2048 + tg * TG: kc * 2048 + (tg + 1) * TG],
                         kc == 0, kc == KC - 1, [wb_] + gRA[kc][tg * 4:tg * 4 + 4], [pub])
                psx, psb = ps_next()
                for hl in range(2):
                    B.mm(psx[:], onesb[0:1, :],
                         bsr[0:1, hl, g * 128:(g + 1) * 128].unsqueeze(1).to_broadcast([1, 4, 128]),
                         hl == 0, False, [bconst, bbsr], [psb])
                for c in range(4):
                    tt = tg * 4 + c
                    B.mm(psx[:, c * 128:(c + 1) * 128], RB[:, tt * 1024 + g * 128: tt * 1024 + (g + 1) * 128],
                         wspT[:, g, :], False, c == 3, fb(gRB, tt * 1024 + g * 128, 128) + [bwsp], [psb])
                ug, ugb, _ = st32.next()
                B.act(ug[:], pu[:], AF.Gelu, [pub], [ugb])
                B.tt(RB[:, AT0 + g * 2048 + tg * TG: AT0 + g * 2048 + (tg + 1) * TG], ug[:], psx[:], ALU.mult,
                     [ugb, psb], fb(gRB, AT0 + g * 2048 + tg * TG, TG))
    if stop == "p1b":
        dbg_dump(RB, gRB, 32768)
        return

    hw = {}

    def load_head_qk(h):
        (wq_,), wqkb = wload([(w_in, 0, D, 2048 + h * 128, 128)])
        i_qk = (slot_i[0] - 1) % NSLOT
        wk_ = wload_into(i_qk, 2048, w_in, 0, D, 3072 + h * 128, 128)
        hw[h] = (wq_, wk_, wqkb)

    def load_head_v(h):
        (wvv,), wvb = wload([(w_in, 0, D, 4096 + h * 128, 128)])
        hw[h] = hw[h] + (wvv, wvb)

    def load_head_w(h):
        load_head_qk(h)
        load_head_v(h)

    def branch_ld(Wbr, gate_c0, j):
        return wload([(w_in, 0, D, gate_c0 + j * 128, 128), (Wbr, 0, GW, j * 128, 128)])

    def branch(Wbr, gate_c0, act_off, second, preloaded=None, tail_hook=None, interleave=None):
        def ld(j):
            return branch_ld(Wbr, gate_c0, j)
        nw = preloaded if preloaded is not None else ld(0)
        for j in range(KC):
            (wg_, wy_), wb_ = nw
            if j + 1 < KC:
                nw = ld(j + 1)
            elif tail_hook is not None:
                tail_hook()
            for tg in range(NTG):
                for _ in range(3):
                    if interleave:
                        interleave.pop(0)()
                if second:
                    mt_, mtb, mtd = st32.next()
                    B.dma("sp", mt_[:], MA[j, :, tg * TG:(tg + 1) * TG], mtd, [bMA[j][tg]], [mtb])
                py, pyb = ps_next()
                for kc in range(8):
                    B.mm(py[:], wy_[:, kc, :], RB[:, act_off + kc * 2048 + tg * TG: act_off + kc * 2048 + (tg + 1) * TG],
                         kc == 0, kc == 7, [wb_] + fb(gRB, act_off + kc * 2048 + tg * TG, TG), [pyb])
                pg, pgb = ps_next()
                for kc in range(KC):
                    B.mm(pg[:], wg_[:, kc, :], RA[:, kc * 2048 + tg * TG: kc * 2048 + (tg + 1) * TG],
                         kc == 0, kc == KC - 1, [wb_] + gRA[kc][tg * 4:tg * 4 + 4], [pgb])
                sg, sgb, sgd = st32.next()
                B.act(sg[:], pg[:], AF.Sigmoid, [pgb], [sgb])
                if not second:
                    B.tt(sg[:], sg[:], py[:], ALU.mult, [sgb, pyb], [sgb])
                    B.dma("pool", MA[j, :, tg * TG:(tg + 1) * TG], sg[:], sgd, [sgb], [bMA[j][tg]])
                else:
                    B.tt(sg[:], sg[:], py[:], ALU.mult, [sgb, pyb], [sgb])
                    ob, obb, obd = stb.next()
                    B.tt(ob[:], sg[:], mt_[:], ALU.add, [sgb, mtb], [obb])
                    B.dma("pool", MG[tg * 4:(tg + 1) * 4, :, j * 128:(j + 1) * 128].rearrange("c p t -> p c t"),
                          ob[:].rearrange("p (c t) -> p c t", c=4), obd, [obb], [bMG[j][tg]])

    rope_tables()
    branch(w_a, 5120, AT0, False, tail_hook=lambda: load_head_qk(0), interleave=rope_ops)
    while rope_ops:
        rope_ops.pop(0)()
    if stop == "p1c":
        return

    cosT, bcos = xt.t[0], xt.b[0]
    sinT, bsin = xt.t[1], xt.b[1]
    B.dma("sp", cosT[:], CS[0], xt.d[0], [bCS], [bcos])
    B.dma("sp", sinT[:], CS[1], xt.d[1], [bCS], [bsin])
    SCALE = float(128 ** -0.5)
    mbts = [(nb.t[0], nb.b[0]), (nb.t[1], nb.b[1])]
    kmts = [(kmt[:, 0:8], bkmt), (kmt[:, 8:16], Buf("kmt1"))]
    pj_i = [0]

    def pj_bank():
        i = 5 + pj_i[0] % 3
        pj_i[0] += 1
        return psum[i], bps[i]

    def offs(h):
        s = h % 2
        return (8 + 3 * s) * 2048, (9 + 3 * s) * 2048, (10 + 3 * s) * 2048

    def proj(h, tg):
        wq_, wk_, wqkb, wvv, wvb = hw[h]
        QO, KO, VO = offs(h)
        for (wmat, dst0) in ((wq_, QO), (wk_, KO)):
            pq, pqb = pj_bank()
            for kc in range(KC):
                B.mm(pq[:], wmat[:, kc, :], RA[:, kc * 2048 + tg * TG: kc * 2048 + (tg + 1) * TG],
                     kc == 0, kc == KC - 1, [wqkb] + gRA[kc][tg * 4:tg * 4 + 4], [pqb])
            dst = RB[:, dst0 + tg * TG: dst0 + (tg + 1) * TG]
            dstb = fb(gRB, dst0 + tg * TG, TG)
            qf, qfb, _ = st32.next()
            B.cp(qf[0:32, :], pq[0:32, :], [pqb], [qfb], eng="act")
            B.cp(dst, pq[:], [pqb], dstb, eng="act")
            pr, prb = pj_bank()
            B.mm(pr[0:32, :], permf[0:32, :], qf[0:32, :], True, True, [bconst, qfb], [prb])
            t1, t1b, _ = st32.next()
            B.tt(t1[0:32, :], pq[0:32, :], cosT[0:32, tg * TG:(tg + 1) * TG], ALU.mult, [pqb, bcos], [t1b])
            B.tt(qf[0:32, :], pr[0:32, :], sinT[0:32, tg * TG:(tg + 1) * TG], ALU.mult, [prb, bsin, qfb], [qfb])
            B.tt(dst[0:32, :], t1[0:32, :], qf[0:32, :], ALU.add, [t1b, qfb], dstb)
        pv_, pvb = pj_bank()
        for c in range(4):
            tt = tg * 4 + c
            for kc in range(KC):
                B.mm(pv_[:, c * 128:(c + 1) * 128], RA[:, kc * 2048 + tt * 128: kc * 2048 + tt * 128 + 128],
                     wvv[:, kc, :], kc == 0, kc == KC - 1, [gRA[kc][tt], wvb], [pvb])
        B.cp(RB[:, VO + tg * TG: VO + (tg + 1) * TG], pv_[:], [pvb], fb(gRB, VO + tg * TG, TG), eng="act")

    gstate = {}

    def gate_a(h):
        QO, KO, VO = offs(h)
        kmt_, kmtb = kmts[h % 2]
        kms, kmsb, _ = sm.next()
        B.S.add("dve", lambda e, kms=kms, KO=KO: e.tensor_reduce(
            kms[:, 0:8], RB[:, KO:KO + 2048].rearrange("p (n k) -> p n k", n=8), AX.X, ALU.add), fb(gRB, KO, 2048), [kmsb])
        B.ts(kmt_, kms[:, 0:8], 1.0 / 256.0, None, ALU.mult, None, [kmsb], [kmtb])
        pgt, pgtb = pj_bank()
        for qt in range(NT):
            B.mm(pgt[:, qt * 8:(qt + 1) * 8], RB[:, QO + qt * 128: QO + (qt + 1) * 128], kmt_, True, True,
                 fb(gRB, QO + qt * 128, 128) + [kmtb], [pgtb])
        gm, gmb, _ = gmt.next()
        B.tt(gm[:], pgt[:, 0:128], ibt[:].rearrange("p a b -> p (a b)"), ALU.add, [pgtb, bconst], [gmb])
        gm3 = gm[:].rearrange("p (a b) -> p a b", a=16)
        B.tt(cmpt.rearrange("p (a n m) -> p a n m", a=16, n=8),
             gm3.unsqueeze(2).to_broadcast([128, 16, 8, 8]), gm3.unsqueeze(3).to_broadcast([128, 16, 8, 8]),
             ALU.is_gt, [gmb], [bcmp])
        cnt, cntb, _ = gmt.next()
        B.S.add("dve", lambda e, cnt=cnt: e.tensor_reduce(cnt[:].rearrange("p (a n) -> p a n", a=16),
                                                          cmpt.rearrange("p (a n m) -> p a n m", a=16, n=8),
                                                          AX.X, ALU.add), [bcmp], [cntb])
        mbb, mbbb, _ = mb.next()
        B.stt(mbb[:], cnt[:], 3.0, nvt[:].rearrange("p a b -> p (a b)"), ALU.is_ge, ALU.mult, [cntb, bconst], [mbbb])
        gstate[h] = (mbb, mbbb)

    def gate_b(h):
        mbb, mbbb = gstate[h]
        mbt_, mbtb = mbts[h % 2]
        for tg in range(NTG):
            pt_, ptb = pj_bank()
            ptv = pt_[:].bitcast(BF16)
            for c in range(4):
                qt = tg * 4 + c
                B.tr(ptv[0:8, c * 128:(c + 1) * 128], mbb[:, qt * 8:(qt + 1) * 8], identb[:], [mbbb, bconst], [ptb])
            B.cp(mbt_[0:8, tg * TG:(tg + 1) * TG], ptv[0:8, 0:512], [ptb], [mbtb])

    sc_i = [0]

    def att(h, tg):
        QO, KO, VO = offs(h)
        mbt_, mbtb = mbts[h % 2]
        po, pob = psum[0], bps[0]
        pd, pdb = psum[1], bps[1]
        nkt = 4 * tg + 4

        def score(kt):
            i = 2 + sc_i[0] % 3
            sc_i[0] += 1
            ps_, psb_ = psum[i], bps[i]
            diag = kt >= 4 * tg
            sel = tg >= 2
            B.mm(ps_[:], RB[:, KO + kt * 128: KO + (kt + 1) * 128], RB[:, QO + tg * TG: QO + (tg + 1) * TG], True,
                 not (sel or diag), fb(gRB, KO + kt * 128, 128) + fb(gRB, QO + tg * TG, TG), [psb_])
            if sel:
                B.mm(ps_[:], eall[0:8, kt // 2, :], mbt_[0:8, tg * TG:(tg + 1) * TG], False, not diag,
                     [bconst, mbtb], [psb_])
            if diag:
                B.mm(ps_[:], identb[:], cm[:, kt - 4 * tg, :], False, True, [bconst], [psb_])
            return ps_, psb_

        LA = 2
        pend = [score(kt) for kt in range(min(LA, nkt))]
        for kt in range(nkt):
            if kt + LA < nkt:
                pend.append(score(kt + LA))
            ps_, psb_ = pend.pop(0)
            pt2, pt2b, _ = stb.next()
            B.act(pt2[:], ps_[:], AF.Exp, [psb_], [pt2b], scale=SCALE)
            B.mm(po[:], RB[:, VO + kt * 128: VO + (kt + 1) * 128], pt2[:], kt == 0, kt == nkt - 1,
                 fb(gRB, VO + kt * 128, 128) + [pt2b], [pob])
            B.mm(pd[:], onesb[:], pt2[:], kt == 0, kt == nkt - 1, [bconst, pt2b], [pdb])
        rc, rcb, _ = st32.next()
        B.S.add("dve", lambda e, rc=rc, pd=pd: e.reciprocal(rc[:], pd[:]), [pdb], [rcb])
        B.tt(RB[:, h * 2048 + tg * TG: h * 2048 + (tg + 1) * TG], po[:], rc[:], ALU.mult, [pob, rcb],
             fb(gRB, h * 2048 + tg * TG, TG))

    load_head_v(0)
    for tg in range(NTG):
        proj(0, tg)
    gate_a(0)
    gate_b(0)
    pre_b = None
    for h in range(8):
        nh = h + 1 if h + 1 < 8 else None
        if nh is not None:
            load_head_w(nh)
        else:
            pre_b = branch_ld(w_b, 7168, 0)
        for tg in range(3):
            att(h, tg)
            if nh is not None:
                proj(nh, tg)
        if nh is not None:
            proj(nh, 3)
            gate_a(nh)
        att(h, 3)
        if nh is not None:
            gate_b(nh)
    if stop == "p1d":
        dbg_dump(RB, gRB, 16384)
        return

    branch(w_b, 7168, 0, True, preloaded=pre_b)
    if stop == "p1e":
        return

    memT = yt.t[0][:].bitcast(BF16)
    gY = [[yt.b[0]] * 16, [yt.b[0]] * 16]
    cmf = cm[:].rearrange("p a b -> p (a b)")
    bKV = Buf("kv")
    KMc, VMc = 0, 1024
    for mtile in range(2):
        t, tb, td = load_tile(mem[mtile * 128:(mtile + 1) * 128, :], [])
        prenorm_T(t[:], tb, 2, memT, gY, 0, 256, mtile * 128)
    for blk in range(2):
        (wk2,), wk2b = wload([(w_ckv, 0, D, blk * 256, 256)])
        for jj in range(2):
            hh = blk * 2 + jj
            p_, pb_ = ps_next()
            for kc in range(KC):
                B.mm(p_[:, 0:256], wk2[:, kc, jj * 128:(jj + 1) * 128], memT[:, kc * 256:(kc + 1) * 256],
                     kc == 0, kc == KC - 1, [wk2b, yt.b[0]], [pb_])
            B.cp(cmf[:, KMc + hh * 256: KMc + (hh + 1) * 256], p_[:, 0:256], [pb_], [bconst, bKV], eng="act")
    for blk in range(2):
        (wv2,), wv2b = wload([(w_ckv, 0, D, 512 + blk * 256, 256)])
        for ktile in range(2):
            p_, pb_ = ps_next()
            for kc in range(KC):
                B.mm(p_[:, 0:256], memT[:, kc * 256 + ktile * 128: kc * 256 + ktile * 128 + 128], wv2[:, kc, :],
                     kc == 0, kc == KC - 1, [wv2b, yt.b[0]], [pb_])
            B.cp(cmf[:, VMc + ktile * 512 + blk * 256: VMc + ktile * 512 + (blk + 1) * 256], p_[:, 0:256], [pb_],
                 [bconst, bKV], eng="act")


    RA3 = RA[:].rearrange("p (k n) -> p k n", k=KC)
    for cb in range(4):
        dwc = B.dsem("wtm%d" % cb)
        for hf in range(2):
            cast_load(RA3[:, hf * 8:(hf + 1) * 8, cb * 512:(cb + 1) * 512], w_tm, hf * 1024, 1024, cb * 512, 512, dwc,
                      [gRA[kc][cb * 4 + s_] for kc in range(hf * 8, hf * 8 + 8) for s_ in range(4)])
    load_gtab(gains[0:1, :])
    ybufs = [(yt.t[0][:], yt.b[0]), (slots[0][:].bitcast(F32), bslot[0])]
    dy3 = [B.dsem("y3l0"), B.dsem("y3l1")]
    mts = [(slots[1][:, 0:2048], Buf("mt0")), (slots[1][:, 2048:4096], Buf("mt1"))]
    dmts = [B.dsem("mt0"), B.dsem("mt1")]
    ybanks = [(psum[i], bps[i]) for i in range(4)]
    tbanks = [[(psum[4], bps[4]), (psum[5], bps[5])], [(psum[6], bps[6]), (psum[7], bps[7])]]

    def load_mT(tt):
        t, tb = mts[tt % 2]
        B.dma("sp", t, MG[tt], dmts[tt % 2], [bMG[k][tt // 4] for k in range(KC)], [tb, bslot[1]])

    def p2a_mm(tt, res):
        if tt + 1 < NT:
            load_mT(tt + 1)
        mt_, mtb = mts[tt % 2]
        for cb in range(4):
            p, pb = ybanks[cb]
            for kc in range(KC):
                B.mm(p[:], mt_[:, kc * 128:(kc + 1) * 128], RA[:, kc * 2048 + cb * 512: kc * 2048 + (cb + 1) * 512],
                     kc == 0, kc == KC - 1, [mtb, bslot[1]] + gRA[kc][cb * 4:cb * 4 + 4], [pb])

    def sublayer_pipe(mm_fn, res_src, res_bufs, Hout, bH, gi, res_tiles, act_identity):
        st = {}

        def load_res(tt):
            t, tb, td = res_tiles[tt % len(res_tiles)]
            B.dma("sp", t[:], res_src[tt * 128:(tt + 1) * 128, :], td, res_bufs(tt), [tb])
            st[tt] = [t, tb, td]

        def S1(tt):
            y, yb = ybufs[tt % 2]
            s, sbf, _ = sm.next()
            for c, (p_, pb_) in enumerate(ybanks):
                jk, jkb, _ = stb.next()
                B.act(jk[:], p_[:], AF.Square, [pb_], [jkb, sbf], accum=s[:, c:c + 1])
                B.act(y[:, c * 512:(c + 1) * 512], p_[:], AF.Identity, [pb_], [yb])
            st[tt] += [y, yb, s, sbf]

        def CH(tt):
            t, tb, td, y, yb, s, sbf = st[tt]
            r, rb = rstd_of(s[:, 0:4], sbf, 4, D)
            B.stt(y, y, r, gtab[:], ALU.mult, ALU.mult, [yb, rb, bgt], [yb])
            B.tt(t[:], y, t[:], ALU.add, [yb, tb], [tb])
            B.dma("pool", Hout[tt * 128:(tt + 1) * 128, :], t[:], td, [tb], [bH[tt]])
            n, nbuf, _ = nb.next()
            s2, s2b, _ = sm.next()
            B.stt(n[:], t[:], 1.0, t[:], ALU.mult, ALU.mult, [tb], [nbuf, s2b], accum=s2[:, 0:1])
            r2, r2b = rstd_of(s2[:, 0:1], s2b, 1, D)
            if act_identity:
                B.act(n[:], t[:], AF.Identity, [tb, r2b], [nbuf], scale=r2)
            else:
                B.ts(n[:], t[:], r2, None, ALU.mult, None, [tb, r2b], [nbuf])
            st[tt] = (n, nbuf)

        def TR(tt):
            n, nbuf = st.pop(tt)
            transpose_out(n, nbuf, gi, RB, gRB, 0, 2048, tt * 128, banks=tbanks[tt % 2])

        nres = len(res_tiles)
        for tt in range(min(nres, NT)):
            load_res(tt)
        mm_fn(0, None)
        S1(0)
        mm_fn(1, None)
        S1(1)
        CH(0)
        if nres < NT:
            load_res(nres)
        for tt in range(NT):
            if tt + 2 < NT:
                mm_fn(tt + 2, None)
                S1(tt + 2)
            if tt + 1 < NT:
                CH(tt + 1)
                if tt + 1 + nres < NT:
                    load_res(tt + 1 + nres)
            TR(tt)

    load_mT(0)
    sublayer_pipe(p2a_mm, x, lambda tt: [], H1, bH1, 1, [(xt.t[i], xt.b[i], xt.d[i]) for i in range(2)], False)
    if stop == "p2a":
        dbg_dump(RB, gRB, 32768)
        return

    QT0, OT0, WO0 = 0, 8192, 16384
    dwo = B.dsem("wo")
    for ch in range(2):
        cast_load(RA[:, WO0:WO0 + 8192].rearrange("p (k n) -> p k n", k=4)[:, :, ch * 1024:(ch + 1) * 1024], w_co, 0, 512,
                  ch * 1024, 1024, dwo, fb(gRA, WO0, 8192))
    for blk in range(2):
        (wq2,), wq2b = wload([(w_cq, 0, D, blk * 256, 256)])
        for jj in range(2):
            hh = blk * 2 + jj
            for tg in range(NTG):
                p, pb = ps_next()
                for kc in range(KC):
                    B.mm(p[:], wq2[:, kc, jj * 128:(jj + 1) * 128], RB[:, kc * 2048 + tg * TG: kc * 2048 + (tg + 1) * TG],
                         kc == 0, kc == KC - 1, [wq2b] + gRB[kc][tg * 4:tg * 4 + 4], [pb])
                B.cp(RA[:, QT0 + hh * 2048 + tg * TG: QT0 + hh * 2048 + (tg + 1) * TG], p[:], [pb],
                     fb(gRA, QT0 + hh * 2048 + tg * TG, TG), eng="act")
    its = [(hh, tg) for hh in range(4) for tg in range(NTG)]

    def ca_scores(idx):
        hh, tg = its[idx]
        res = []
        for kt in range(2):
            bi = 4 + (idx % 2) * 2 + kt
            ps_, psb_ = psum[bi], bps[bi]
            B.mm(ps_[:], cmf[:, KMc + hh * 256 + kt * 128: KMc + hh * 256 + (kt + 1) * 128],
                 RA[:, QT0 + hh * 2048 + tg * TG: QT0 + hh * 2048 + (tg + 1) * TG], True, True,
                 [bKV] + fb(gRA, QT0 + hh * 2048 + tg * TG, TG), [psb_])
            res.append((ps_, psb_))
        return res

    nsc = ca_scores(0)
    for idx, (hh, tg) in enumerate(its):
        sc = nsc
        if idx + 1 < len(its):
            nsc = ca_scores(idx + 1)
        po, pob = psum[(idx % 2) * 2], bps[(idx % 2) * 2]
        pd, pdb = psum[(idx % 2) * 2 + 1], bps[(idx % 2) * 2 + 1]
        pts = []
        for kt in range(2):
            pt2, pt2b, _ = stb.next()
            B.act(pt2[:], sc[kt][0][:], AF.Exp, [sc[kt][1]], [pt2b], scale=SCALE)
            pts.append((pt2, pt2b))
        for kt in range(2):
            pt2, pt2b = pts[kt]
            B.mm(po[:], cmf[:, VMc + kt * 512 + hh * 128: VMc + kt * 512 + (hh + 1) * 128], pt2[:], kt == 0, kt == 1,
                 [bKV, pt2b], [pob])
            B.mm(pd[:], onesb[:], pt2[:], kt == 0, kt == 1, [bconst, pt2b], [pdb])
        rc, rcb, _ = st32.next()
        B.S.add("dve", lambda e, rc=rc, pd=pd: e.reciprocal(rc[:], pd[:]), [pdb], [rcb])
        B.tt(RA[:, OT0 + hh * 2048 + tg * TG: OT0 + hh * 2048 + (tg + 1) * TG], po[:], rc[:], ALU.mult, [pob, rcb],
             fb(gRA, OT0 + hh * 2048 + tg * TG, TG))
    load_gtab(gains[1:2, :])

    def ca_mm(tt, res):
        for cb in range(4):
            p, pb = ybanks[cb]
            for hh in range(4):
                B.mm(p[:], RA[:, OT0 + hh * 2048 + tt * 128: OT0 + hh * 2048 + tt * 128 + 128],
                     RA[:, WO0 + hh * 2048 + cb * 512: WO0 + hh * 2048 + (cb + 1) * 512], hh == 0, hh == 3,
                     fb(gRA, OT0 + hh * 2048 + tt * 128, 128) + fb(gRA, WO0 + hh * 2048 + cb * 512, 512), [pb])

    ex = [RA[:, QT0 + i * 4096: QT0 + (i + 1) * 4096].bitcast(F32) for i in range(2)]
    exb = [Buf("ex0"), Buf("ex1")]
    exd = [B.dsem("ex0"), B.dsem("ex1")]

    class _T:
        def __init__(self, ap):
            self.ap = ap

        def __getitem__(self, k):
            return self.ap
    B.S.add("dve", lambda e: e.memset(ex[0][:, 0:1], 0.0), [], fb(gRA, QT0, 8192) + [exb[0], exb[1]])
    res4 = [(xt.t[0], xt.b[0], xt.d[0]), (xt.t[1], xt.b[1], xt.d[1]), (_T(ex[0]), exb[0], exd[0]), (_T(ex[1]), exb[1], exd[1])]
    sublayer_pipe(ca_mm, H1, lambda tt: [bH1[tt]], H2, bH2, 3, res4, True)
    if stop == "p2b":
        dbg_dump(RB, gRB, 32768)
        return

    def ldf(j):
        return wload([(w_fg, 0, D, j * 128, 128), (w_fu, 0, D, j * 128, 128)])

    def load_wd_chunk(cb, r, i):
        R, g = (RA, gRA) if r == 0 else (RB, gRB)
        dstv = R[:, i * 5632:(i + 1) * 5632].rearrange("p (k n) -> p k n", k=11)
        cast_load(dstv, w_fd, i * 1408, 1408, cb * 512, 512, B.dsem("wd%d_%d" % (cb, i)), fb(g, i * 5632, 5632))

    def load_wd(cb, r):
        for i in range(4):
            load_wd_chunk(cb, r, i)
    nw = ldf(0)
    B.S.add("dve", lambda e: e.memset(RA[:, 0:2], 0.0), [], [exb[0], exb[1]] + fb(gRA, 0, 8192))
    for j in range(HC):
        (wg_, wu_), wb_ = nw
        if j + 1 < HC:
            nw = ldf(j + 1)
        if j in (2, 4, 6, 8):
            load_wd_chunk(0, 0, (j - 2) // 2)
        for tg in range(NTG):
            pg, pgb = ps_next()
            for kc in range(KC):
                B.mm(pg[:], wg_[:, kc, :], RB[:, kc * 2048 + tg * TG: kc * 2048 + (tg + 1) * TG],
                     kc == 0, kc == KC - 1, [wb_] + gRB[kc][tg * 4:tg * 4 + 4], [pgb])
            pu, pub = ps_next()
            for kc in range(KC):
                B.mm(pu[:], wu_[:, kc, :], RB[:, kc * 2048 + tg * TG: kc * 2048 + (tg + 1) * TG],
                     kc == 0, kc == KC - 1, [wb_] + gRB[kc][tg * 4:tg * 4 + 4], [pub])
            sg, sgb, _ = st32.next()
            B.act(sg[:], pg[:], AF.Silu, [pgb], [sgb])
            ob, obb, obd = stb.next()
            B.tt(ob[:], sg[:], pu[:], ALU.mult, [sgb, pub], [obb])
            B.dma("pool", HIDT[tg * 4:(tg + 1) * 4, :, j * 128:(j + 1) * 128].rearrange("c p t -> p c t"),
                  ob[:].rearrange("p (c t) -> p c t", c=4), obd, [obb], [bHID[j][tg]])
    if stop == "p3a":
        return

    HT0 = 22528
    regs = ((RA, gRA), (RB, gRB))
    dhts = [B.dsem("ht0"), B.dsem("ht1")]

    hidx = [0]

    def load_hT(tt):
        i = hidx[0] % 2
        hidx[0] += 1
        R, g = regs[i]
        B.dma("sp", R[:, HT0:HT0 + 5632], HIDT[tt], dhts[i], [bHID[k][tt // 4] for k in range(HC)], fb(g, HT0, 5632))
        return R, g

    def down_mm(p, pb, HR, hgd, WR, wgd):
        for k in range(HC):
            B.mm(p[:], HR[:, HT0 + k * 128: HT0 + (k + 1) * 128], WR[:, k * 512:(k + 1) * 512], k == 0, k == HC - 1,
                 fb(hgd, HT0 + k * 128, 128) + fb(wgd, k * 512, 512), [pb])

    load_wd(1, 1)
    seq = [(cb, tt) for cb in range(2) for tt in range(NT)] + [(2, tt) for tt in range(NT)]
    nxt_h = load_hT(seq[0][1])
    for si, (cb, tt) in enumerate(seq):
        HR, hgd = nxt_h
        if si + 1 < len(seq):
            nxt_h = load_hT(seq[si + 1][1])
        if cb < 2:
            if cb == 1 and tt == 0:
                load_wd(2, 0)
            WR, wgd = regs[cb]
            p, pb = ps_next()
            down_mm(p, pb, HR, hgd, WR, wgd)
            s_, sb_, sd_ = st32.next()
            B.cp(s_[:], p[:], [pb], [sb_], eng="act")
            B.dma("pool", Y3[tt * 128:(tt + 1) * 128, cb * 512:(cb + 1) * 512], s_[:], sd_, [sb_], [bY3[tt]])
        else:
            if tt == 0:
                load_wd(3, 1)
                load_gtab(gains[2:3, :])
            ht_, htb, htd = load_tile(H2[tt * 128:(tt + 1) * 128, :], [bH2[tt]])
            y, yb = ybufs[tt % 2]
            B.dma("sp", y[:, 0:1024], Y3[tt * 128:(tt + 1) * 128, :], dy3[tt % 2], [bY3[tt]], [yb])
            pa = ps_next()
            down_mm(pa[0], pa[1], HR, hgd, RA, gRA)
            pb2 = ps_next()
            down_mm(pb2[0], pb2[1], HR, hgd, RB, gRB)
            n, nbuf, _ = nb.next()
            s, sbf = post_evac([pa, pb2], y, yb, n, nbuf, col0=1024)
            B.act(n[:, 1024:2048], y[:, 0:1024], AF.Square, [yb], [nbuf, sbf], accum=s[:, 2:3])
            post_apply(s, sbf, 3, y, yb, ht_[:], htb)
            B.dma("pool", out[tt * 128:(tt + 1) * 128, :], ht_[:], htd, [htb], [Buf("o")])
def _consts():
    bf = ml_dtypes.bfloat16
    c = {}
    c["c_identb"] = np.eye(128, dtype=np.float32).astype(bf)
    c["c_identf"] = np.eye(128, dtype=np.float32)
    c["c_onesb"] = np.ones((128, 128), dtype=np.float32).astype(bf)
    s = np.arange(128)
    c["c_tri"] = (s[:, None] <= s[None, :]).astype(np.float32)
    k = np.arange(128)[:, None, None]
    j = np.arange(4)[None, :, None]
    q = np.arange(512)[None, None, :]
    c["c_cm"] = np.where((j * 128 + k) > q, NEG, 0.0).astype(np.float32).astype(bf).reshape(128, 2048)
    e = np.zeros((8, 8, 128), np.float32)
    for n in range(8):
        e[n, n, :] = 1.0
    c["c_eall"] = e.astype(bf).reshape(8, 1024)
    qt = np.arange(16)[:, None]
    m = np.arange(8)[None, :]
    own = qt // 2
    ib = np.where(m >= own, -1e30, 0.0).astype(np.float32)
    nv = np.where(m < own, NEG, 0.0).astype(np.float32)
    c["c_ib"] = np.broadcast_to(ib.reshape(1, 128), (128, 128)).copy()
    c["c_nv"] = np.broadcast_to(nv.reshape(1, 128), (128, 128)).copy()
    col = np.zeros((128, 4), np.float32)
    inv_freq = (1.0 / (np.float32(500000.0) ** (np.arange(0, 32, 2, dtype=np.float32) / np.float32(32)))).astype(np.float32)
    inv64 = 1.0 / (500000.0 ** (np.arange(0, 32, 2, dtype=np.float64) / 32.0))
    inv_hi = inv64.astype(np.float32)
    inv_lo = (inv64 - inv_hi.astype(np.float64)).astype(np.float32)
    col[0:16, 0] = inv_hi
    col[16:32, 0] = inv_hi
    col[0:16, 2] = inv_lo
    col[16:32, 2] = inv_lo
    col[0:16, 1] = -1.0
    col[16:32, 1] = 1.0
    c["c_col"] = col
    pm = np.zeros((128, 32), np.float32)
    for m in range(16):
        pm[m + 16, m] = -1.0
        pm[m, m + 16] = 1.0
    c["c_perm"] = pm
    return c


_NC_CACHE = {}


def _get_nc(dbg=False, stop=None):
    key = (dbg, stop)
    if key not in _NC_CACHE:
        _NC_CACHE[key] = build_program(dbg, stop)
    return _NC_CACHE[key]


def make_in_maps(inputs):
    f = lambda a: np.ascontiguousarray(np.asarray(a))
    c = _consts()
    gains = np.stack([f(inputs[k])[0] for k in ("tm_post_g", "ca_post_g", "ffn_post_g")]).astype(np.float32)
    gcols = np.concatenate([f(inputs[k])[0].reshape(16, 128).T for k in ("tm_pre_g", "ca_pre_g", "mem_norm_g",
                                                                          "ffn_pre_g")], axis=1).astype(np.float32)
    gcols = np.ascontiguousarray(gcols)
    lnv = np.concatenate([f(inputs["ln_v_g"])[0], f(inputs["ln_v_b"])[0]]).reshape(1, 2048).astype(np.float32)
    shared = {
        "w_in": f(inputs["w_in"])[0], "w_sp": f(inputs["w_spatial"])[0],
        "b_sp": f(inputs["b_spatial"])[0].reshape(1, 1024),
        "w_a": f(inputs["w_branch_a"])[0], "w_b": f(inputs["w_branch_b"])[0], "w_tm": f(inputs["w_tm_out"])[0],
        "w_cq": f(inputs["w_ca_q"])[0], "w_ckv": f(inputs["w_ca_kv"])[0], "w_co": f(inputs["w_ca_out"])[0],
        "w_fg": f(inputs["w_ffn_gate"])[0], "w_fu": f(inputs["w_ffn_up"])[0], "w_fd": f(inputs["w_ffn_down"])[0],
        "gains": gains, "lnv": lnv, "gcols": gcols,
    }
    shared.update(c)
    xs = f(inputs["x"])
    mems = f(inputs["mem"])
    ps = f(inputs["positions"]).astype(np.int32)
    maps = []
    for b in range(8):
        m = dict(shared)
        m["x"] = xs[b]
        m["mem"] = mems[b]
        m["pos"] = ps[b].reshape(1, SEQ)
        maps.append(m)
    return maps


def kernel(**inputs):
    nc = _get_nc()
    maps = make_in_maps(inputs)
    res = run_bass_kernel_spmd(nc, maps, core_ids=list(range(8)))
    return np.stack([np.asarray(r["out"]) for r in res.results], axis=0).astype(np.float32)
```
